# Optimizing a Trainium2 kernel written in Bass

```python
import math
import jax, jax.numpy as jnp
from jax import lax
import numpy as np

D_MODEL = 1024
BATCH = 16
SEQ = 2048
DEPTH = 4
DEC_BATCH = 16
DEC_SEQ = 16
PAST_LEN = 4096

CHUNK = 64
N_MIXERS = 3
N_A = (DEPTH + 2) // 3
N_B = (DEPTH + 1) // 3
N_C = DEPTH // 3

A_HEAD_DIM = 64
A_HEADS = D_MODEL // (2 * A_HEAD_DIM)
A_Q_BLOCK = 128
ROPE_THETA = 10000.0

B_WIDTH = D_MODEL
B_BLOCKS = 4
B_BLOCK_W = B_WIDTH // B_BLOCKS
B_CONV_W = 4
B_C = 8.0

C_HEADS = 16
C_HEAD_DIM = D_MODEL // C_HEADS
C_LEFT_CHUNKS = 8
C_BAND_PAST = C_LEFT_CHUNKS * CHUNK
C_REL_CLIP = 128

D_FF = 4 * D_MODEL
EPS = 1e-6

kernel_name = "hybrid_streaming_encoder_step"


def rms_norm(x, g):
    xf = x.astype(jnp.float32)
    y = xf * lax.rsqrt(jnp.mean(xf * xf, axis=-1, keepdims=True) + EPS)
    return (y * g.astype(jnp.float32)).astype(x.dtype)


def rope(x, pos):
    half = x.shape[-1] // 2
    inv = ROPE_THETA ** (-jnp.arange(half, dtype=jnp.float32) / half)
    ang = pos.astype(jnp.float32)[:, None] * inv[None, :]
    shape = (1, pos.shape[0]) + (1,) * (x.ndim - 3) + (half,)
    cos = jnp.cos(ang).reshape(shape)
    sin = jnp.sin(ang).reshape(shape)
    x1 = x[..., :half].astype(jnp.float32)
    x2 = x[..., half:].astype(jnp.float32)
    return jnp.concatenate([x1 * cos - x2 * sin, x2 * cos + x1 * sin], axis=-1).astype(x.dtype)


def diff_core(q, k, v, lam, mask):
    s = jnp.einsum("bqhcd,bkhcd->bhcqk", q, k).astype(jnp.float32) * (A_HEAD_DIM ** -0.5)
    if mask is not None:
        s = jnp.where(mask, s, -jnp.inf)
    p = jax.nn.softmax(s, axis=-1)
    w = p[:, :, 0] - lam * p[:, :, 1]
    return jnp.einsum("bhqk,bkhe->bqhe", w.astype(v.dtype), v)


def diff_attn_prompt(q, k, v, lam):
    B, T = q.shape[:2]
    nb = T // A_Q_BLOCK
    qb = jnp.moveaxis(q.reshape(B, nb, A_Q_BLOCK, A_HEADS, 2, A_HEAD_DIM), 1, 0)
    k_chunk = jnp.arange(T) // CHUNK

    def block(args):
        q_blk, b = args
        q_chunk = (b * A_Q_BLOCK + jnp.arange(A_Q_BLOCK)) // CHUNK
        return diff_core(q_blk, k, v, lam, k_chunk[None, :] <= q_chunk[:, None])

    o = lax.map(block, (qb, jnp.arange(nb)))
    return jnp.moveaxis(o, 0, 1).reshape(B, T, A_HEADS, 2 * A_HEAD_DIM)


def diff_project(h, pos, w_in, q_g, k_g):
    B, T, _ = h.shape
    q, k, v = jnp.split(h @ w_in, 3, axis=-1)
    q = rope(rms_norm(q.reshape(B, T, A_HEADS, 2, A_HEAD_DIM), q_g), pos)
    k = rope(rms_norm(k.reshape(B, T, A_HEADS, 2, A_HEAD_DIM), k_g), pos)
    return q, k, v.reshape(B, T, A_HEADS, 2 * A_HEAD_DIM)


def diff_output(o, subln_g, lam_init, w_out):
    B, T = o.shape[:2]
    o = rms_norm(o, subln_g) * (1.0 - lam_init)
    return o.reshape(B, T, -1) @ w_out


def diff_mixer(hp, hs, cache_k, cache_v, w_in, q_g, k_g, lam_p, subln_g, w_out, lam_init):
    past = cache_k.shape[1]
    lp = lam_p.astype(jnp.float32)
    lam = jnp.exp(jnp.sum(lp[0] * lp[1])) - jnp.exp(jnp.sum(lp[2] * lp[3])) + lam_init
    qp, kp, vp = diff_project(hp, jnp.arange(hp.shape[1]), w_in, q_g, k_g)
    yp = diff_output(diff_attn_prompt(qp, kp, vp, lam), subln_g, lam_init, w_out)
    qs, ks, vs = diff_project(hs, past + jnp.arange(hs.shape[1]), w_in, q_g, k_g)
    k_all = jnp.concatenate([cache_k, ks], axis=1)
    v_all = jnp.concatenate([cache_v, vs], axis=1)
    ys = diff_output(diff_core(qs, k_all, v_all, lam, None), subln_g, lam_init, w_out)
    return yp, ys, kp, vp, ks, vs


def linear_scan(a, b, h0):
    b = b.at[:, 0].add(a[:, 0] * h0)

    def combine(l, r):
        return l[0] * r[0], r[0] * l[1] + r[1]

    return lax.associative_scan(combine, (a, b), axis=1)[1]


def rglru_block(h, conv_hist, h0, w_in, b_in, conv_w, conv_b, ga_w, ga_b, gx_w, gx_b, lam, w_out):
    B, T, _ = h.shape
    gate, u = jnp.split(h @ w_in + b_in, 2, axis=-1)
    gate = jax.nn.gelu(gate)
    u_ext = jnp.concatenate([conv_hist.astype(u.dtype), u], axis=1)
    xc = sum((u_ext[:, j:j + T] * conv_w[j] for j in range(B_CONV_W)), conv_b)
    xb = xc.reshape(B, T, B_BLOCKS, B_BLOCK_W)
    r = jax.nn.sigmoid(jnp.einsum("btnc,ncd->btnd", xb, ga_w).reshape(B, T, B_WIDTH) + ga_b).astype(jnp.float32)
    i = jax.nn.sigmoid(jnp.einsum("btnc,ncd->btnd", xb, gx_w).reshape(B, T, B_WIDTH) + gx_b).astype(jnp.float32)
    log_a = -B_C * r * jax.nn.softplus(-lam.astype(jnp.float32))
    a = jnp.exp(log_a)
    b = jnp.sqrt(-jnp.expm1(2.0 * log_a)) * (i * xc.astype(jnp.float32))
    hs = linear_scan(a, b, h0.astype(jnp.float32))
    y = (hs.astype(h.dtype) * gate) @ w_out
    return y, u_ext[:, -(B_CONV_W - 1):], hs[:, -1].astype(h0.dtype)


def rel_bias(table, rel):
    idx = jnp.clip(rel, -C_REL_CLIP, C_REL_CLIP) + C_REL_CLIP
    return jnp.moveaxis(table[idx], -1, 0).astype(jnp.float32)


def band_core(q, k, v, bias, valid):
    s = jnp.einsum("bqhd,bkhd->bhqk", q, k).astype(jnp.float32) * (C_HEAD_DIM ** -0.5) + bias
    if valid is not None:
        s = jnp.where(valid, s, -jnp.inf)
    p = jax.nn.softmax(s, axis=-1)
    return jnp.einsum("bhqk,bkhd->bqhd", p.astype(v.dtype), v)


def band_attn_prompt(q, k, v, table):
    B, T = q.shape[:2]
    nc = T // CHUNK
    band = C_BAND_PAST + CHUNK
    pad = ((0, 0), (C_BAND_PAST, 0), (0, 0), (0, 0))
    kp = jnp.pad(k, pad)
    vp = jnp.pad(v, pad)
    qc = jnp.moveaxis(q.reshape(B, nc, CHUNK, C_HEADS, C_HEAD_DIM), 1, 0)
    key_off = jnp.arange(band) - C_BAND_PAST
    bias = rel_bias(table, jnp.arange(CHUNK)[:, None] - key_off[None, :])

    def one_chunk(args):
        q_c, c = args
        start = c * CHUNK
        kb = lax.dynamic_slice_in_dim(kp, start, band, axis=1)
        vb = lax.dynamic_slice_in_dim(vp, start, band, axis=1)
        valid = (start + key_off) >= 0
        return band_core(q_c, kb, vb, bias, valid)

    o = lax.map(one_chunk, (qc, jnp.arange(nc)))
    return jnp.moveaxis(o, 0, 1).reshape(B, T, C_HEADS * C_HEAD_DIM)


def band_mixer(hp, hs, cache_k, cache_v, w_in, q_g, k_g, table, w_out):
    def project(h):
        B, T, _ = h.shape
        q, k, v = jnp.split(h @ w_in, 3, axis=-1)
        shp = (B, T, C_HEADS, C_HEAD_DIM)
        return rms_norm(q.reshape(shp), q_g), rms_norm(k.reshape(shp), k_g), v.reshape(shp)

    qp, kp, vp = project(hp)
    yp = band_attn_prompt(qp, kp, vp, table) @ w_out
    qs, ks, vs = project(hs)
    Bs, Ts = hs.shape[:2]
    W = cache_k.shape[1]
    key_pos = jnp.concatenate([jnp.arange(W) - W, jnp.arange(Ts)])
    bias = rel_bias(table, jnp.arange(Ts)[:, None] - key_pos[None, :])
    o_s = band_core(qs, jnp.concatenate([cache_k, ks], axis=1),
                    jnp.concatenate([cache_v, vs], axis=1), bias, None)
    ys = o_s.reshape(Bs, Ts, -1) @ w_out
    keep = min(C_BAND_PAST, hp.shape[1])
    return yp, ys, kp[:, -keep:], vp[:, -keep:], ks, vs


def sq_relu_mlp(h, w1, w2):
    return jnp.square(jax.nn.relu(h @ w1)) @ w2


def setup_inputs(seed: int = 0) -> dict:
    key = jax.random.key(seed)
    keys = jax.random.split(key, 40)
    counter = iter(range(40))

    def nrm(shape, scale):
        return jax.random.normal(keys[next(counter)], shape, jnp.float32) * scale

    def gain(shape):
        return 1.0 + nrm(shape, 0.02)

    c_win = min(C_BAND_PAST, PAST_LEN)
    d = D_MODEL
    lam_u = jax.random.uniform(keys[next(counter)], (N_B, B_WIDTH), jnp.float32, 0.9, 0.999)
    sig = lam_u ** (1.0 / B_C)
    b_lambda = jnp.log(sig) - jnp.log1p(-sig)
    return {
        "x_prompt": nrm((BATCH, SEQ, d), 1.0),
        "x_sample": nrm((DEC_BATCH, DEC_SEQ, d), 1.0),
        "cache_a_k": nrm((N_A, DEC_BATCH, PAST_LEN, A_HEADS, 2, A_HEAD_DIM), 1.0),
        "cache_a_v": nrm((N_A, DEC_BATCH, PAST_LEN, A_HEADS, 2 * A_HEAD_DIM), 1.0),
        "state_b_conv": nrm((N_B, DEC_BATCH, B_CONV_W - 1, B_WIDTH), 1.0),
        "state_b_h": nrm((N_B, DEC_BATCH, B_WIDTH), 0.5),
        "cache_c_k": nrm((N_C, DEC_BATCH, c_win, C_HEADS, C_HEAD_DIM), 1.0),
        "cache_c_v": nrm((N_C, DEC_BATCH, c_win, C_HEADS, C_HEAD_DIM), 1.0),
        "norm_mix_g": gain((DEPTH, d)),
        "norm_mlp_g": gain((DEPTH, d)),
        "norm_final_g": gain((d,)),
        "a_w_in": nrm((N_A, d, 3 * d), d ** -0.5),
        "a_q_norm_g": gain((N_A, A_HEAD_DIM)),
        "a_k_norm_g": gain((N_A, A_HEAD_DIM)),
        "a_lambda": nrm((N_A, 4, A_HEAD_DIM), 0.1),
        "a_subln_g": gain((N_A, 2 * A_HEAD_DIM)),
        "a_w_out": nrm((N_A, d, d), d ** -0.5),
        "b_w_in": nrm((N_B, d, 2 * B_WIDTH), d ** -0.5),
        "b_b_in": nrm((N_B, 2 * B_WIDTH), 0.02),
        "b_conv_w": nrm((N_B, B_CONV_W, B_WIDTH), B_CONV_W ** -0.5),
        "b_conv_b": nrm((N_B, B_WIDTH), 0.02),
        "b_gate_a_w": nrm((N_B, B_BLOCKS, B_BLOCK_W, B_BLOCK_W), B_BLOCK_W ** -0.5),
        "b_gate_a_b": nrm((N_B, B_WIDTH), 0.02),
        "b_gate_x_w": nrm((N_B, B_BLOCKS, B_BLOCK_W, B_BLOCK_W), B_BLOCK_W ** -0.5),
        "b_gate_x_b": nrm((N_B, B_WIDTH), 0.02),
        "b_lambda": b_lambda,
        "b_w_out": nrm((N_B, B_WIDTH, d), B_WIDTH ** -0.5),
        "c_w_in": nrm((N_C, d, 3 * d), d ** -0.5),
        "c_q_norm_g": gain((N_C, C_HEAD_DIM)),
        "c_k_norm_g": gain((N_C, C_HEAD_DIM)),
        "c_rel_bias": nrm((N_C, 2 * C_REL_CLIP + 1, C_HEADS), 0.2),
        "c_w_out": nrm((N_C, d, d), d ** -0.5),
        "mlp_w1": nrm((DEPTH, d, D_FF), d ** -0.5),
        "mlp_w2": nrm((DEPTH, D_FF, d), 0.5 * D_FF ** -0.5),
    }


def reference(x_prompt, x_sample, cache_a_k, cache_a_v, state_b_conv, state_b_h, cache_c_k, cache_c_v,
              norm_mix_g, norm_mlp_g, norm_final_g,
              a_w_in, a_q_norm_g, a_k_norm_g, a_lambda, a_subln_g, a_w_out,
              b_w_in, b_b_in, b_conv_w, b_conv_b, b_gate_a_w, b_gate_a_b, b_gate_x_w, b_gate_x_b,
              b_lambda, b_w_out,
              c_w_in, c_q_norm_g, c_k_norm_g, c_rel_bias, c_w_out,
              mlp_w1, mlp_w2):
    xp, xs = x_prompt, x_sample
    a_kp, a_vp, a_ks, a_vs = [], [], [], []
    b_cp, b_hp, b_cs, b_hs = [], [], [], []
    c_kp, c_vp, c_ks, c_vs = [], [], [], []
    for layer in range(DEPTH):
        kind = layer % N_MIXERS
        idx = layer // N_MIXERS
        hp = rms_norm(xp, norm_mix_g[layer])
        hs = rms_norm(xs, norm_mix_g[layer])
        if kind == 0:
            lam_init = 0.8 - 0.6 * math.exp(-0.3 * layer)
            yp, ys, kp, vp, ks, vs = diff_mixer(
                hp, hs, cache_a_k[idx], cache_a_v[idx], a_w_in[idx], a_q_norm_g[idx], a_k_norm_g[idx],
                a_lambda[idx], a_subln_g[idx], a_w_out[idx], lam_init)
            a_kp.append(kp); a_vp.append(vp); a_ks.append(ks); a_vs.append(vs)
        elif kind == 1:
            blk = (b_w_in[idx], b_b_in[idx], b_conv_w[idx], b_conv_b[idx], b_gate_a_w[idx], b_gate_a_b[idx],
                   b_gate_x_w[idx], b_gate_x_b[idx], b_lambda[idx], b_w_out[idx])
            Bp = hp.shape[0]
            yp, cp, h_p = rglru_block(hp, jnp.zeros((Bp, B_CONV_W - 1, B_WIDTH), hp.dtype),
                                      jnp.zeros((Bp, B_WIDTH), state_b_h.dtype), *blk)
            ys, cs, h_s = rglru_block(hs, state_b_conv[idx], state_b_h[idx], *blk)
            b_cp.append(cp); b_hp.append(h_p); b_cs.append(cs); b_hs.append(h_s)
        else:
            yp, ys, kp, vp, ks, vs = band_mixer(
                hp, hs, cache_c_k[idx], cache_c_v[idx], c_w_in[idx], c_q_norm_g[idx], c_k_norm_g[idx],
                c_rel_bias[idx], c_w_out[idx])
            c_kp.append(kp); c_vp.append(vp); c_ks.append(ks); c_vs.append(vs)
        xp = xp + yp
        xs = xs + ys
        xp = xp + sq_relu_mlp(rms_norm(xp, norm_mlp_g[layer]), mlp_w1[layer], mlp_w2[layer])
        xs = xs + sq_relu_mlp(rms_norm(xs, norm_mlp_g[layer]), mlp_w1[layer], mlp_w2[layer])
    y_prompt = rms_norm(xp, norm_final_g)
    y_sample = rms_norm(xs, norm_final_g)
    return (y_prompt, y_sample,
            jnp.stack(a_kp), jnp.stack(a_vp), jnp.stack(a_ks), jnp.stack(a_vs),
            jnp.stack(b_cp), jnp.stack(b_hp), jnp.stack(b_cs), jnp.stack(b_hs),
            jnp.stack(c_kp), jnp.stack(c_vp), jnp.stack(c_ks), jnp.stack(c_vs))
```

```python
import math
from contextlib import ExitStack
import numpy as np
import concourse.bass as bass
import concourse.mybir as mybir
from concourse.bass_utils import run_bass_kernel_spmd

F32 = mybir.dt.float32
BF16 = mybir.dt.bfloat16
AF = mybir.ActivationFunctionType
ALU = mybir.AluOpType
AX = mybir.AxisListType

NCORES = 8
D = 1024
T = 2048
TS = 16
PAST = 4096
NTOK = 2 * T + 2 * TS
SCOL = 2 * T
EPS = 1e-6
NEG = -30000.0
import os
DEBUG = bool(int(os.environ.get("KDEBUG", "0")))
NLAYERS = int(os.environ.get("KLAYERS", "4"))
NPASS = int(os.environ.get("KPASS", "99"))
KCORES = int(os.environ.get("KCORES", "8"))
KTILES = os.environ.get("KTILES", "")
KSTOP = int(os.environ.get("KSTOP", "99"))
KS2 = int(os.environ.get("KS2", "99"))
KS3 = int(os.environ.get("KS3", "0"))
SAMPLE_FIRST = int(os.environ.get("KSF", "1"))
POOLENG = os.environ.get("KPOOL", "pool")


class Buf:
    __slots__ = ("name", "w", "r", "excl")

    def __init__(self, name, excl=False):
        self.name = name
        self.w = []
        self.r = []
        self.excl = excl


class Op:
    __slots__ = ("eng", "fn", "deps", "sig", "dma", "ndma", "sem", "val")


class Prog:
    ENG = ("pe", "act", "dve", "pool", "sp")
    NDS = 12

    def __init__(self):
        self.streams = {e: [] for e in self.ENG}
        self.dmaops = []
        self.pending = {e: [] for e in self.ENG}
        self.dfinal = [0] * self.NDS

    def op(self, eng, fn, reads=(), writes=(), dma=False, ndma=1):
        deps = []
        for b in reads:
            deps += b.w
            if b.excl:
                deps += [r for r in b.r if r.eng != eng]
        for b in writes:
            deps += b.w
            deps += b.r
        deps += self.pending[eng]
        self.pending[eng] = []
        o = Op()
        o.eng = eng
        o.fn = fn
        o.dma = dma
        o.ndma = ndma
        o.sig = dma
        o.sem = None
        o.val = 0
        if dma:
            if len(self.dmaops) >= self.NDS:
                deps.append(self.dmaops[-self.NDS])
            self.dmaops.append(o)
        dd = []
        seen = set()
        for d in deps:
            if id(d) in seen:
                continue
            seen.add(id(d))
            if d.eng == "pe" and eng == "pe" and (not d.dma) and (not dma):
                continue
            dd.append(d)
        o.deps = dd
        self.streams[eng].append(o)
        wset = set(id(b) for b in writes)
        for b in reads:
            if id(b) in wset:
                continue
            b.r = [r for r in b.r if (r.dma or r.eng != eng)] + [o]
        for b in writes:
            b.w = [o]
            b.r = []
        return o

    def barrier(self):
        deps = []
        for e in self.ENG:
            for o in reversed(self.streams[e]):
                if not o.dma:
                    deps.append(o)
                    break
        deps += self.dmaops[-self.NDS:]
        for e in self.ENG:
            self.pending[e] = self.pending[e] + deps

    def finalize(self):
        for e in self.ENG:
            for o in self.streams[e]:
                for d in o.deps:
                    d.sig = True
        for e in self.ENG:
            c = 0
            for o in self.streams[e]:
                if o.dma:
                    continue
                if o.sig:
                    c += 1
                    o.sem = ("E", e)
                    o.val = c
        for i, o in enumerate(self.dmaops):
            k = i % self.NDS
            self.dfinal[k] += 16 * o.ndma
            o.sem = ("D", k)
            o.val = self.dfinal[k]

    def emit(self, eng, e, sems):
        waited = {}
        for o in self.streams[eng]:
            for d in o.deps:
                if waited.get(d.sem, 0) < d.val:
                    e.wait_ge(sems[d.sem], d.val)
                    waited[d.sem] = d.val
            r = o.fn(e)
            if o.dma:
                insts = r if isinstance(r, (list, tuple)) else [r]
                assert len(insts) == o.ndma, (len(insts), o.ndma)
                for ins in insts:
                    ins.then_inc(sems[o.sem], 16)
            elif o.sig:
                r.then_inc(sems[o.sem], 1)
        if eng == "sp":
            for k in range(self.NDS):
                if self.dfinal[k] > 0:
                    e.wait_ge(sems[("D", k)], self.dfinal[k])


class Arena:
    def __init__(self, t, words):
        self.t = t
        self.words = words
        self.off = 0

    def reset(self, off=0):
        self.off = off

    def alloc(self, dtype, npart, *shape):
        n = 1
        for s in shape:
            n *= s
        w = n if dtype == F32 else (n + 1) // 2
        w = (w + 1) // 2 * 2
        assert self.off + w <= self.words, ("SBUF arena overflow", self.off, w, self.words)
        a = self.t[0:npart, self.off:self.off + w]
        self.off += w
        if dtype != F32:
            a = a.bitcast(dtype)
        a = a[:, 0:n]
        if len(shape) == 2:
            a = a.rearrange("p (a b) -> p a b", b=shape[1])
        elif len(shape) == 3:
            a = a.rearrange("p (a b c) -> p a b c", b=shape[1], c=shape[2])
        return a


class K:
    def __init__(self):
        self.nc = bass.Bass("TRN2", target_bir_lowering=False)
        self.P = Prog()
        self.bufid = 0

    def buf(self, name="b"):
        self.bufid += 1
        return Buf(f"{name}{self.bufid}")

    def mm(self, out, lhsT, rhs, start, stop, reads, writes):
        self.P.op("pe", lambda e: e.matmul(out, lhsT=lhsT, rhs=rhs, start=start, stop=stop,
                                           skip_group_check=True), reads, writes)

    def tr(self, out, in_, ident, reads, writes):
        self.P.op("pe", lambda e: e.transpose(out, in_, ident), reads, writes)

    def act(self, out, in_, func, reads, writes, scale=None, bias=None):
        kw = {}
        if scale is not None:
            kw["scale"] = scale
        if bias is not None:
            kw["bias"] = bias
        self.P.op("act", lambda e: e.activation(out=out, in_=in_, func=func, **kw), reads, writes)

    def tt(self, eng, out, in0, in1, op, reads, writes):
        self.P.op(eng, lambda e: e.tensor_tensor(out=out, in0=in0, in1=in1, op=op), reads, writes)

    def ts(self, eng, out, in0, s1, s2, op0, op1, reads, writes):
        if op1 is None:
            self.P.op(eng, lambda e: e.tensor_scalar(out=out, in0=in0, scalar1=s1, scalar2=None, op0=op0),
                      reads, writes)
        else:
            self.P.op(eng, lambda e: e.tensor_scalar(out=out, in0=in0, scalar1=s1, scalar2=s2, op0=op0, op1=op1),
                      reads, writes)

    def stt(self, out, in0, scalar, in1, op0, op1, reads, writes):
        self.P.op("dve", lambda e: e.scalar_tensor_tensor(out=out, in0=in0, scalar=scalar, in1=in1,
                                                          op0=op0, op1=op1), reads, writes)

    def cp(self, eng, out, in_, reads, writes):
        if eng == "act":
            self.P.op("act", lambda e: e.copy(out=out, in_=in_), reads, writes)
        else:
            self.P.op(eng, lambda e: e.tensor_copy(out=out, in_=in_), reads, writes)

    def memset(self, eng, ap, val, writes):
        self.P.op(eng, lambda e: e.memset(ap, val), (), writes)

    def dma(self, out, in_, reads, writes, slow=False):
        if slow:
            self.P.op("sp", lambda e: e.dma_start(out=out, in_=in_, allow_slow_non_contiguous=True),
                      reads, writes, dma=True)
        else:
            self.P.op("sp", lambda e: e.dma_start(out=out, in_=in_), reads, writes, dma=True)

    def build(self):
        nc = self.nc

        def din(name, shape, dt=F32):
            return nc.dram_tensor(name, list(shape), dt, kind="ExternalInput").ap()

        def dout(name, shape, dt=F32):
            return nc.dram_tensor(name, list(shape), dt, kind="ExternalOutput").ap()

        def dscr(name, shape, dt=F32):
            return nc.dram_tensor(name, list(shape), dt, kind="Internal").ap()

        I = {}
        I["x_prompt"] = din("x_prompt", (2, T, D))
        I["x_sample"] = din("x_sample", (2, TS, D))
        I["cache_a_k"] = din("cache_a_k", (2, 2, PAST, D))
        I["cache_a_v"] = din("cache_a_v", (2, 2, PAST, D))
        I["state_b_conv"] = din("state_b_conv", (1, 2, 3, D))
        I["state_b_h"] = din("state_b_h", (1, 2, D))
        I["cache_c_k"] = din("cache_c_k", (1, 2, 512, D))
        I["cache_c_v"] = din("cache_c_v", (1, 2, 512, D))
        for nm, shp in [("norm_mix_g", (4, D)), ("norm_mlp_g", (4, D)), ("norm_final_g", (D,)),
                        ("a_w_in", (2, D, 3 * D)), ("a_q_norm_g", (2, 64)), ("a_k_norm_g", (2, 64)),
                        ("a_lambda", (2, 256)), ("a_subln_g", (2, 128)), ("a_w_out", (2, D, D)),
                        ("b_w_in", (1, D, 2 * D)), ("b_b_in", (1, 2 * D)), ("b_conv_w", (1, 4, D)),
                        ("b_conv_b", (1, D)), ("b_gate_a_w", (1, 4, 256, 256)), ("b_gate_a_b", (1, D)),
                        ("b_gate_x_w", (1, 4, 256, 256)), ("b_gate_x_b", (1, D)), ("b_lambda", (1, D)),
                        ("b_w_out", (1, D, D)), ("c_w_in", (1, D, 3 * D)), ("c_q_norm_g", (1, 64)),
                        ("c_k_norm_g", (1, 64)), ("c_rel_bias", (1, 257, 16)), ("c_w_out", (1, D, D)),
                        ("mlp_w1", (4, D, 4 * D)), ("mlp_w2", (4, 4 * D, D)),
                        ("k_ident", (128, 128)), ("k_jmat", (128, 128)), ("k_ones", (128, 128)), ("k_bones", (128, 128)),
                        ("k_rmat", (128, 128)), ("k_onesp", (128, 256)),
                        ("k_cos", (128, 2 * T + 32)), ("k_sin", (128, 2 * T + 32))]:
            I[nm] = din(nm, shp)
        O = {}
        O["y_prompt"] = dout("y_prompt", (2, T, D))
        O["y_sample"] = dout("y_sample", (2, TS, D))
        O["a_k_p"] = dout("a_k_p", (2, 2, T, D))
        O["a_v_p"] = dout("a_v_p", (2, 2, T, D))
        O["a_k_s"] = dout("a_k_s", (2, 2, TS, D))
        O["a_v_s"] = dout("a_v_s", (2, 2, TS, D))
        O["b_conv_p"] = dout("b_conv_p", (1, 2, 3, D))
        O["b_h_p"] = dout("b_h_p", (1, 2, D))
        O["b_conv_s"] = dout("b_conv_s", (1, 2, 3, D))
        O["b_h_s"] = dout("b_h_s", (1, 2, D))
        O["c_k_p"] = dout("c_k_p", (1, 2, 512, D))
        O["c_v_p"] = dout("c_v_p", (1, 2, 512, D))
        O["c_k_s"] = dout("c_k_s", (1, 2, TS, D))
        O["c_v_s"] = dout("c_v_s", (1, 2, TS, D))
        self.I, self.O = I, O
        S = {}
        S["xs"] = (dout if DEBUG else dscr)("s_xs", (8, 128, NTOK))
        S["qs"] = dscr("s_qs", (8, 128, NTOK), BF16)
        S["ks"] = dscr("s_ks", (8, 128, NTOK), BF16)
        S["vs"] = dscr("s_vs", (NTOK, D), BF16)
        S["ext"] = dscr("s_ext", (16, 512))
        self.S = S
        self.dbuf = {k: Buf("dram_" + k) for k in list(S.keys())}

        with ExitStack() as es:
            AW = 50 * 1024
            arena_t = es.enter_context(nc.sbuf_tensor("arena", [128, AW], F32))
            ps_t = es.enter_context(nc.psum_tensor("ps", [128, 8, 512], F32))
            self.ar = Arena(arena_t, AW)
            self.bank = [ps_t[:, b, :] for b in range(8)]
            self.bankb = [Buf(f"bank{b}", excl=True) for b in range(8)]
            sems = {}
            for e in Prog.ENG:
                if e != "sp":
                    sems[("E", e)] = es.enter_context(nc.semaphore("s_" + e))
            for k in range(Prog.NDS):
                sems[("D", k)] = es.enter_context(nc.semaphore(f"s_d{k}"))

            self.record()
            self.P.finalize()
            P = self.P
            block = es.enter_context(nc.Block())

            @block.sync
            def _(e):
                P.emit("sp", e, sems)

            @block.tensor
            def _(e):
                P.emit("pe", e, sems)

            @block.scalar
            def _(e):
                P.emit("act", e, sems)

            @block.vector
            def _(e):
                P.emit("dve", e, sems)

            @block.gpsimd
            def _(e):
                P.emit("pool", e, sems)
        return nc

    def tiles(self):
        r = []
        for s in range(2):
            for t in range(4):
                r.append((s * T + t * 512, 512, s, t))
        r.append((SCOL, 32, -1, 0))
        if KTILES:
            r = [r[int(i)] for i in KTILES.split(",")]
        return r

    def load_consts(self):
        ar = self.ar
        I = self.I
        c = {}
        st = ar.alloc(F32, 128, 128)
        stb = self.buf("cst")
        self.ident = ar.alloc(F32, 128, 128)
        self.cb = self.buf("consts")
        self.dma(self.ident, I["k_ident"][:, :], (), (self.cb,))
        self.jmat = ar.alloc(F32, 128, 128)
        self.dma(self.jmat, I["k_jmat"][:, :], (), (self.cb,))
        self.identb = ar.alloc(BF16, 128, 128)
        self.onesb = ar.alloc(BF16, 128, 128)
        self.bonesb = ar.alloc(BF16, 128, 128)
        self.rmatb = ar.alloc(BF16, 128, 128)
        self.onespb = ar.alloc(BF16, 128, 256)
        self.cp("dve", self.identb, self.ident, (self.cb,), (self.cb,))
        for dst, nm in [(self.onesb, "k_ones"), (self.bonesb, "k_bones"), (self.rmatb, "k_rmat")]:
            self.dma(st, I[nm][:, :], (), (stb,))
            self.cp("dve", dst, st, (stb,), (self.cb,))
        st2 = ar.alloc(F32, 128, 256)
        self.dma(st2, I["k_onesp"][:, :], (), (stb,))
        self.cp("dve", self.onespb, st2, (stb,), (self.cb,))
        self.gmix = ar.alloc(F32, 128, 4, 8)
        self.gmlp = ar.alloc(F32, 128, 4, 8)
        self.gfin = ar.alloc(F32, 128, 8)
        for l in range(4):
            self.dma(self.gmix[:, l, :], I["norm_mix_g"][l].rearrange("(c p) -> p c", p=128), (), (self.cb,), slow=True)
            self.dma(self.gmlp[:, l, :], I["norm_mlp_g"][l].rearrange("(c p) -> p c", p=128), (), (self.cb,), slow=True)
        self.dma(self.gfin, I["norm_final_g"].rearrange("(c p) -> p c", p=128), (), (self.cb,), slow=True)
        self.persist = ar.off

    def load_weight(self, src2d, dst, Kdim, Ncols, stg, stgb):
        engs = ("pool", "dve", "act")
        i = getattr(self, "_wl_i", 0)
        piece = 2048
        for kc in range(Kdim // 128):
            for c0 in range(0, Ncols, piece):
                n = min(piece, Ncols - c0)
                s = stg[i % len(stg)]
                sb = stgb[i % len(stg)]
                self.dma(s[:, 0:n], src2d[kc * 128:(kc + 1) * 128, c0:c0 + n], (), (sb,))
                self.cp(engs[i % 3], dst[:, kc, c0:c0 + n], s[:, 0:n], (sb,), (self.wb,))
                i += 1
        self._wl_i = i

    def rstd(self, out, ss, inv_n, reads, wbuf):
        self.act(out, ss, AF.Ln, reads, (wbuf,), scale=inv_n, bias=self.epsc)
        self.act(out, out, AF.Exp, (wbuf,), (wbuf,), scale=-0.5)

    def next_bank(self, pool):
        i = self._bk.get(pool, 0)
        lst = self._pools[pool]
        self._bk[pool] = i + 1
        return lst[i % len(lst)]

    def norm(self, xT, xb, N, gcol, hT, hb, sq, sqb, rs, rsb):
        self.act(sq[:, :, 0:N], xT[:, :, 0:N], AF.Square, (xb,), (sqb,))
        b = self.next_bank("g")
        for c in range(8):
            self.mm(self.bank[b][:, 0:N], self.onesb, sq[:, c, 0:N], c == 0, c == 7, (sqb, self.cb), (self.bankb[b],))
        self.rstd(rs[:, 0:N], self.bank[b][:, 0:N], 1.0 / D, (self.bankb[b],), rsb)
        for c in range(8):
            self.stt(hT[:, c, 0:N], xT[:, c, 0:N], gcol[:, c:c + 1], rs[:, 0:N], ALU.mult, ALU.mult,
                     (xb, rsb, self.cb), (hb,))

    def load_x(self, tile, xT, xb, first, xtm=None, xtmb=None):
        col0, N, s, t = tile
        if not first:
            self.dma(xT[:, :, 0:N], self.S["xs"][:, :, col0:col0 + N].rearrange("c p n -> p c n"),
                     (self.dbuf["xs"],), (xb,))
            return
        if s >= 0:
            for u in range(4):
                self.dma(xtm[:, u, :], self.I["x_prompt"][s, t * 512 + u * 128: t * 512 + (u + 1) * 128, :], (), (xtmb,))
            nsub, nt = 4, 128
        else:
            self.dma(xtm[0:32, 0, :], self.I["x_sample"].rearrange("s t d -> (s t) d"), (), (xtmb,))
            nsub, nt = 1, 32
        for c in range(8):
            b = self.next_bank("g")
            for u in range(nsub):
                self.tr(self.bank[b][:, u * 128:u * 128 + nt], xtm[0:nt, u, c * 128:(c + 1) * 128],
                        self.ident[0:nt, 0:nt], (xtmb, self.cb), (self.bankb[b],))
            self.cp("act" if c % 2 else "dve", xT[:, c, 0:N], self.bank[b][:, 0:N], (self.bankb[b],), (xb,))
        self.dma(self.S["xs"][:, :, col0:col0 + N].rearrange("c p n -> p c n"), xT[:, :, 0:N], (xb,), (self.dbuf["xs"],))

    def store_x(self, tile, xT, xb):
        col0, N, s, t = tile
        self.dma(self.S["xs"][:, :, col0:col0 + N].rearrange("c p n -> p c n"), xT[:, :, 0:N], (xb,), (self.dbuf["xs"],))

    def out_tm(self, src, srcb, N, tm, tmb, dst_rows_fn):
        nsub = (N + 127) // 128
        nt = min(N, 128)
        for u in range(nsub):
            for half in range(2):
                b = self.next_bank("g")
                for i in range(4):
                    self.tr(self.bank[b][0:nt, i * 128:(i + 1) * 128], src[:, half * 4 + i, u * 128:u * 128 + nt],
                            self.ident, (srcb, self.cb), (self.bankb[b],))
                self.cp("act" if half else "dve", tm[0:nt, u, half * 512:(half + 1) * 512], self.bank[b][0:nt, :],
                        (self.bankb[b],), (tmb,))
        for u in range(nsub):
            for (dst, r0, r1) in dst_rows_fn(u, nt):
                self.dma(dst, tm[r0:r1, u, :], (tmb,), ())

    def record(self):
        self._bk = {}
        self._pools = {"g": [0, 1, 2, 3, 4, 5, 6, 7]}
        self.epsc = EPS
        self.ar.reset(0)
        self.load_consts()
        self.epscol = self.ar.alloc(F32, 128, 1)
        self.memset("dve", self.epscol, EPS, (self.cb,))
        self.epsc = self.epscol
        self.persist = self.ar.off
        np_ = [0]

        def go():
            np_[0] += 1
            return np_[0] <= NPASS
        for layer in range(NLAYERS):
            kind = layer % 3
            idx = layer // 3
            if kind == 0:
                if go():
                    self.pass_proj(layer, "a", idx)
                if go():
                    self.pass_attn_a(layer, idx)
            elif kind == 1:
                if go():
                    self.pass_b(layer)
            else:
                if go():
                    self.pass_proj(layer, "c", 0)
                if go():
                    self.pass_attn_c(layer)
            if go():
                self.pass_mlp(layer)

    def phase(self):
        self.P.barrier()
        self.ar.reset(self.persist)
        self.wb = self.buf("w")

    def pass_proj(self, layer, kind, idx):
        I, O, S, ar = self.I, self.O, self.S, self.ar
        self.phase()
        self._pools = {"g": [0, 1, 2, 3, 4, 5, 6, 7]}
        wsrc = I["a_w_in"][idx] if kind == "a" else I["c_w_in"][0]
        Wb = ar.alloc(BF16, 128, 8, 3 * D)
        stg = [ar.alloc(F32, 128, 2048) for _ in range(2)]
        stgb = [self.buf("stg") for _ in range(2)]
        mark = ar.off - 2 * 2048
        self.load_weight(wsrc, Wb, D, 3 * D, stg, stgb)
        self.P.barrier()
        ar.reset(mark)
        if KSTOP < 1:
            return
        vb = self.buf("vec")
        gq = ar.alloc(F32, 128, 1)
        gk = ar.alloc(F32, 128, 1)
        gqp = ar.alloc(F32, 128, 1)
        gkp = ar.alloc(F32, 128, 1)
        qg = (I["a_q_norm_g"][idx] if kind == "a" else I["c_q_norm_g"][0]).rearrange("(d o) -> d o", o=1)
        kg = (I["a_k_norm_g"][idx] if kind == "a" else I["c_k_norm_g"][0]).rearrange("(d o) -> d o", o=1)
        for h in range(2):
            self.dma(gq[h * 64:(h + 1) * 64, :], qg, (), (vb,))
            self.dma(gk[h * 64:(h + 1) * 64, :], kg, (), (vb,))
            if kind == "a":
                self.dma(gqp[h * 64:h * 64 + 32, :], qg[32:64, :], (), (vb,))
                self.dma(gqp[h * 64 + 32:h * 64 + 64, :], qg[0:32, :], (), (vb,))
                self.dma(gkp[h * 64:h * 64 + 32, :], kg[32:64, :], (), (vb,))
                self.dma(gkp[h * 64 + 32:h * 64 + 64, :], kg[0:32, :], (), (vb,))
        if kind == "a":
            cosT = ar.alloc(F32, 128, 512)
            sinT = ar.alloc(F32, 128, 512)
            csb = self.buf("cs")
        nxb = 1 if layer == 0 else 2
        xTs = [ar.alloc(F32, 128, 8, 512) for _ in range(nxb)]
        xbs = [self.buf("xT") for _ in range(nxb)]
        xtm = ar.alloc(F32, 128, 4, D) if layer == 0 else None
        xtmb = self.buf("xtm")
        hT = ar.alloc(BF16, 128, 8, 512)
        hb = self.buf("hT")
        sq = ar.alloc(BF16, 128, 8, 512)
        sqb = self.buf("sq")
        rs = ar.alloc(F32, 128, 512)
        rsb = self.buf("rs")
        kpost = ar.alloc(F32, 128, 8, 512)
        kpb = self.buf("kpost")
        tm = ar.alloc(F32, 128, 4, D)
        tmb = self.buf("tm")
        vtm = ar.alloc(F32, 128, 4, D)
        vtmb = self.buf("vtm")
        vbf = ar.alloc(BF16, 128, 4, D)
        vbfb = self.buf("vbf")
        qT = ar.alloc(BF16, 128, 8, 512)
        qTb = self.buf("qT")
        kT = ar.alloc(BF16, 128, 8, 512)
        kTb = self.buf("kT")
        scr = []
        for i in range(2):
            scr.append(dict(raw=ar.alloc(BF16, 128, 512), sq=ar.alloc(BF16, 128, 512), rs=ar.alloc(F32, 128, 512),
                            t1=ar.alloc(F32, 128, 512), t2=ar.alloc(F32, 128, 512),
                            rawb=self.buf("raw"), sqb=self.buf("sq"), rsb=self.buf("rs"), t1b=self.buf("t1"),
                            t2b=self.buf("t2")))
        gmix = self.gmix[:, layer, :]
        ci = 0
        tl_ = self.tiles()
        if nxb == 2:
            self.load_x(tl_[0], xTs[0], xbs[0], False)
        for ti, tile in enumerate(tl_):
            col0, N, s, t = tile
            xT, xb = xTs[ti % nxb], xbs[ti % nxb]
            if KSTOP < 2:
                return
            if nxb == 2:
                if ti + 1 < len(tl_):
                    self.load_x(tl_[ti + 1], xTs[(ti + 1) % 2], xbs[(ti + 1) % 2], False)
            else:
                self.load_x(tile, xT, xb, layer == 0, xtm, xtmb)
            if KSTOP < 3:
                continue
            if kind == "a":
                tc0 = col0 if s >= 0 else 2 * T
                self.dma(cosT[:, 0:N], I["k_cos"][:, tc0:tc0 + N], (), (csb,))
                self.dma(sinT[:, 0:N], I["k_sin"][:, tc0:tc0 + N], (), (csb,))
            self.norm(xT, xb, N, gmix, hT, hb, sq, sqb, rs, rsb)
            if KSTOP < 4:
                continue
            for which in range(2):
                for j in range(8):
                    sc = scr[ci % 2]
                    ci += 1
                    b = self.next_bank("g")
                    R = self.bank[b][:, 0:N]
                    Rb = self.bankb[b]
                    for kc in range(8):
                        self.mm(R, Wb[:, kc, which * D + j * 128: which * D + (j + 1) * 128], hT[:, kc, 0:N],
                                kc == 0, kc == 7, (self.wb, hb), (Rb,))
                    gcol = gq if which == 0 else gk
                    gpcol = gqp if which == 0 else gkp
                    self.act(sc["sq"][:, 0:N], R, AF.Square, (Rb,), (sc["sqb"],))
                    b2 = self.next_bank("g")
                    self.mm(self.bank[b2][:, 0:N], self.bonesb, sc["sq"][:, 0:N], True, True, (sc["sqb"], self.cb),
                            (self.bankb[b2],))
                    self.rstd(sc["rs"][:, 0:N], self.bank[b2][:, 0:N], 1.0 / 64, (self.bankb[b2],), sc["rsb"])
                    if which == 0:
                        dst_f = None
                        dst_b = qT[:, j, 0:N]
                        dstb_buf = qTb
                    else:
                        dst_f = kpost[:, j, 0:N]
                        dst_b = kT[:, j, 0:N]
                        dstb_buf = kTb
                    if kind == "a":
                        self.cp("act", sc["raw"][:, 0:N], R, (Rb,), (sc["rawb"],))
                        b3 = self.next_bank("g")
                        self.mm(self.bank[b3][:, 0:N], self.rmatb, sc["raw"][:, 0:N], True, True,
                                (sc["rawb"], self.cb), (self.bankb[b3],))
                        self.stt(sc["t1"][:, 0:N], R, gcol[:, 0:1], cosT[:, 0:N], ALU.mult, ALU.mult,
                                 (Rb, vb, csb), (sc["t1b"],))
                        self.stt(sc["t2"][:, 0:N], self.bank[b3][:, 0:N], gpcol[:, 0:1], sinT[:, 0:N], ALU.mult, ALU.mult,
                                 (self.bankb[b3], vb, csb), (sc["t2b"],))
                        self.tt("dve", sc["t1"][:, 0:N], sc["t1"][:, 0:N], sc["t2"][:, 0:N], ALU.add,
                                (sc["t1b"], sc["t2b"]), (sc["t1b"],))
                        if which == 0:
                            self.tt("dve", dst_b, sc["t1"][:, 0:N], sc["rs"][:, 0:N], ALU.mult,
                                    (sc["t1b"], sc["rsb"]), (dstb_buf,))
                        else:
                            self.tt("dve", dst_f, sc["t1"][:, 0:N], sc["rs"][:, 0:N], ALU.mult,
                                    (sc["t1b"], sc["rsb"]), (kpb,))
                            self.cp("act", dst_b, dst_f, (kpb,), (dstb_buf,))
                    else:
                        if which == 0:
                            self.stt(dst_b, R, gcol[:, 0:1], sc["rs"][:, 0:N], ALU.mult, ALU.mult,
                                     (Rb, vb, sc["rsb"]), (dstb_buf,))
                        else:
                            self.stt(dst_f, R, gcol[:, 0:1], sc["rs"][:, 0:N], ALU.mult, ALU.mult,
                                     (Rb, vb, sc["rsb"]), (kpb,))
                            self.cp("act", dst_b, dst_f, (kpb,), (dstb_buf,))
            if KSTOP < 5:
                continue
            self.dma(S["qs"][:, :, col0:col0 + N].rearrange("c p n -> p c n"), qT[:, :, 0:N], (qTb,), (self.dbuf["qs"],))
            self.dma(S["ks"][:, :, col0:col0 + N].rearrange("c p n -> p c n"), kT[:, :, 0:N], (kTb,), (self.dbuf["ks"],))
            nsub = (N + 127) // 128
            nt = min(N, 128)
            for u in range(nsub):
                for half in range(2):
                    b = self.next_bank("g")
                    for kc in range(8):
                        self.mm(self.bank[b][0:nt, :], hT[:, kc, u * 128:u * 128 + nt],
                                Wb[:, kc, 2 * D + half * 512: 2 * D + (half + 1) * 512], kc == 0, kc == 7,
                                (self.wb, hb), (self.bankb[b],))
                    self.cp("act", vtm[0:nt, u, half * 512:(half + 1) * 512], self.bank[b][0:nt, :],
                            (self.bankb[b],), (vtmb,))
                    self.cp("dve", vbf[0:nt, u, half * 512:(half + 1) * 512], self.bank[b][0:nt, :],
                            (self.bankb[b],), (vbfb,))
            for u in range(nsub):
                self.dma(S["vs"][col0 + u * 128: col0 + u * 128 + nt, :], vbf[0:nt, u, :], (vbfb,), (self.dbuf["vs"],))
            if KSTOP < 6:
                continue
            want = True
            if kind == "c" and s >= 0 and t != 3:
                want = False
            if want:
                if kind == "a":
                    okp, ovp, oks, ovs = O["a_k_p"][idx], O["a_v_p"][idx], O["a_k_s"][idx], O["a_v_s"][idx]
                else:
                    okp, ovp, oks, ovs = O["c_k_p"][0], O["c_v_p"][0], O["c_k_s"][0], O["c_v_s"][0]

                def rows(dp, ds):
                    def fn(u, nt_):
                        if s >= 0:
                            if kind == "a":
                                r0 = t * 512 + u * 128
                            else:
                                r0 = u * 128
                            return [(dp[s, r0:r0 + 128, :], 0, 128)]
                        return [(ds[0, :, :], 0, 16), (ds[1, :, :], 16, 32)]
                    return fn
                self.out_tm(kpost, kpb, N, tm, tmb, rows(okp, oks))
                for u in range(nsub):
                    for (dst, r0, r1) in rows(ovp, ovs)(u, nt):
                        self.dma(dst, vtm[r0:r1, u, :], (vtmb,), ())

    def pass_attn_a(self, layer, idx):
        I, O, S, ar = self.I, self.O, self.S, self.ar
        self.phase()
        self._pools = {"s": [0, 1, 2, 3], "acc": [4, 5, 6, 7], "g": [0, 1, 2, 3]}
        lam_init = 0.8 - 0.6 * math.exp(-0.3 * layer)
        Wo = ar.alloc(BF16, 128, 8, D)
        stg = [ar.alloc(F32, 128, 2048) for _ in range(2)]
        stgb = [self.buf("stg") for _ in range(2)]
        mark = ar.off - 2 * 2048
        self.load_weight(I["a_w_out"][idx], Wo, D, D, stg, stgb)
        self.P.barrier()
        ar.reset(mark)
        vb = self.buf("vec")
        lp = ar.alloc(F32, 128, 256)
        self.dma(lp, I["a_lambda"][idx].partition_broadcast(128), (), (vb,))
        pr = ar.alloc(F32, 128, 128)
        s12 = ar.alloc(F32, 128, 2)
        neglam = ar.alloc(F32, 128, 1)
        self.tt("dve", pr[:, 0:64], lp[:, 0:64], lp[:, 64:128], ALU.mult, (vb,), (vb,))
        self.tt("dve", pr[:, 64:128], lp[:, 128:192], lp[:, 192:256], ALU.mult, (vb,), (vb,))
        self.P.op("dve", lambda e: e.tensor_reduce(out=s12, in_=pr.rearrange("p (a b) -> p a b", b=64), axis=AX.X,
                                                  op=ALU.add), (vb,), (vb,))
        self.act(s12, s12, AF.Exp, (vb,), (vb,))
        self.tt("dve", neglam, s12[:, 1:2], s12[:, 0:1], ALU.subtract, (vb,), (vb,))
        self.ts("dve", neglam, neglam, -lam_init, None, ALU.add, None, (vb,), (vb,))
        gsub = ar.alloc(F32, 128, 1)
        self.dma(gsub, I["a_subln_g"][idx].rearrange("(d o) -> d o", o=1), (), (vb,))
        self.ts("dve", gsub, gsub, 1.0 - lam_init, None, ALU.mult, None, (vb,), (vb,))

        if KSTOP < 1:
            return
        KT = ar.alloc(BF16, 128, 8, T)
        KTb = [self.buf("KT") for _ in range(4)]
        V = ar.alloc(BF16, 128, 16, D)
        Vb_ = [self.buf("V") for _ in range(4)]
        QTl = [ar.alloc(BF16, 128, 8, 512) for _ in range(2)]
        QTbl = [self.buf("QT") for _ in range(2)]
        xT = ar.alloc(F32, 128, 8, 512)
        xb = self.buf("xT")
        attnT = ar.alloc(BF16, 128, 8, 512)
        atb = self.buf("attnT")
        PT = [ar.alloc(BF16, 128, 512) for _ in range(4)]
        PTb = [self.buf("PT") for _ in range(4)]
        ep = dict(r0=ar.alloc(F32, 128, 512), r1=ar.alloc(F32, 128, 512), A=ar.alloc(F32, 128, 512),
                  Bm=ar.alloc(F32, 128, 512), sq=ar.alloc(BF16, 128, 512), rs=ar.alloc(F32, 128, 512))
        epb = {k: self.buf("ep_" + k) for k in ep}

        def epilogue(N, O0, O0b, S0, S0b, O1, O1b, S1, S1b, dst, dstb):
            self.act(ep["r0"][:, 0:N], S0, AF.Ln, (S0b,), (epb["r0"],))
            self.act(ep["r0"][:, 0:N], ep["r0"][:, 0:N], AF.Exp, (epb["r0"],), (epb["r0"],), scale=-1.0)
            self.act(ep["r1"][:, 0:N], S1, AF.Ln, (S1b,), (epb["r1"],))
            self.act(ep["r1"][:, 0:N], ep["r1"][:, 0:N], AF.Exp, (epb["r1"],), (epb["r1"],), scale=-1.0)
            self.tt("dve", ep["A"][:, 0:N], O0, ep["r0"][:, 0:N], ALU.mult, (O0b, epb["r0"]), (epb["A"],))
            self.tt("dve", ep["Bm"][:, 0:N], O1, ep["r1"][:, 0:N], ALU.mult, (O1b, epb["r1"]), (epb["Bm"],))
            self.stt(ep["A"][:, 0:N], ep["Bm"][:, 0:N], neglam[:, 0:1], ep["A"][:, 0:N], ALU.mult, ALU.add,
                     (epb["Bm"], epb["A"], vb), (epb["A"],))
            self.act(ep["sq"][:, 0:N], ep["A"][:, 0:N], AF.Square, (epb["A"],), (epb["sq"],))
            b = self.next_bank("s")
            self.mm(self.bank[b][:, 0:N], self.onesb, ep["sq"][:, 0:N], True, True, (epb["sq"], self.cb), (self.bankb[b],))
            self.rstd(ep["rs"][:, 0:N], self.bank[b][:, 0:N], 1.0 / 128, (self.bankb[b],), epb["rs"])
            self.stt(dst, ep["A"][:, 0:N], gsub[:, 0:1], ep["rs"][:, 0:N], ALU.mult, ALU.mult,
                     (epb["A"], epb["rs"], vb), (dstb,))

        def outproj(tile, N):
            for co in range(8):
                b = self.next_bank("s")
                for j in range(8):
                    self.mm(self.bank[b][:, 0:N], Wo[:, j, co * 128:(co + 1) * 128], attnT[:, j, 0:N], j == 0, j == 7,
                            (self.wb, atb), (self.bankb[b],))
                self.tt("dve", xT[:, co, 0:N], self.bank[b][:, 0:N], xT[:, co, 0:N], ALU.add, (self.bankb[b], xb), (xb,))
            self.store_x(tile, xT, xb)

        tiles = self.tiles()
        stt_ = dict(pti=0)

        def do_prompts():
            pti = stt_["pti"]
            for s in range(2):
                for t in range(4):
                    tile = tiles[s * 4 + t]
                    col0, N = tile[0], 512

                    def kvq_loads(ix, kv=True, q=True):
                        t2 = ix % 4
                        c0 = tiles[ix][0]
                        if kv:
                          self.dma(KT[:, :, t2 * 512:(t2 + 1) * 512], S["ks"][:, :, c0:c0 + 512].rearrange("c p n -> p c n"),
                                 (self.dbuf["ks"],), (KTb[t2],))
                          self.dma(V[:, 4 * t2:4 * t2 + 4, :], S["vs"][c0:c0 + 512, :].rearrange("(u p) d -> p u d", p=128),
                                 (self.dbuf["vs"],), (Vb_[t2],))
                        if q:
                          self.dma(QTl[ix % 2], S["qs"][:, :, c0:c0 + 512].rearrange("c p n -> p c n"), (self.dbuf["qs"],),
                                 (QTbl[ix % 2],))
                    ix_ = s * 4 + t
                    if ix_ == 0:
                        kvq_loads(0)
                    if t == 0 and s > 0:
                        kvq_loads(ix_, kv=True, q=False)
                    self.load_x(tile, xT, xb, False)
                    if ix_ + 1 < 8:
                        kvq_loads(ix_ + 1, kv=(t < 3), q=True)
                    QT, QTb = QTl[ix_ % 2], QTbl[ix_ % 2]
                    if KSTOP < 2:
                        continue
                    nk = 4 * t + 4
                    for j in range(8):
                        acc = [self.next_bank("acc") for _ in range(4)]
                        for c in range(2):
                            Ob, Sb_ = acc[2 * c], acc[2 * c + 1]
                            pr0 = c * 64

                            def qk(kt):
                                i = kt - 4 * t
                                q0 = 128 * i if i > 0 else 0
                                b = self.next_bank("s")
                                self.mm(self.bank[b][:, q0:512], KT[pr0:pr0 + 64, j, kt * 128:(kt + 1) * 128],
                                        QT[pr0:pr0 + 64, j, q0:512], True, True, (KTb[kt // 4], QTb), (self.bankb[b],))
                                return b, q0
                            pend = [qk(0)]
                            if nk > 1:
                                pend.append(qk(1))
                            for kt in range(nk):
                                b, q0 = pend.pop(0)
                                pt = PT[pti % 4]
                                ptb = PTb[pti % 4]
                                pti += 1
                                self.act(pt[:, q0:512], self.bank[b][:, q0:512], AF.Exp, (self.bankb[b],), (ptb,), scale=0.125)
                                i = kt - 4 * t
                                if i >= 0:
                                    self.memset("dve", pt[64:128, 128 * i:128 * i + 64], 0.0, (ptb,))
                                if kt + 2 < nk:
                                    pend.append(qk(kt + 2))
                                self.mm(self.bank[Ob][:, q0:512], V[:, kt, j * 128:(j + 1) * 128], pt[:, q0:512],
                                        kt == 0, kt == nk - 1, (Vb_[kt // 4], ptb), (self.bankb[Ob],))
                                self.mm(self.bank[Sb_][:, q0:512], self.onesb, pt[:, q0:512],
                                        kt == 0, kt == nk - 1, (ptb, self.cb), (self.bankb[Sb_],))
                        if KSTOP < 4:
                            continue
                        epilogue(512, self.bank[acc[0]], self.bankb[acc[0]], self.bank[acc[1]], self.bankb[acc[1]],
                                 self.bank[acc[2]], self.bankb[acc[2]], self.bank[acc[3]], self.bankb[acc[3]],
                                 attnT[:, j, :], atb)
                    if KSTOP < 5:
                        continue
                    outproj(tile, 512)

            stt_["pti"] = pti

        def do_sample():
            pti = stt_["pti"]
            tile = tiles[8]
            self.load_x(tile, xT, xb, False)
            kst = [ar.alloc(F32, 128, D) for _ in range(2)]
            kstb = [self.buf("kst") for _ in range(2)]
            vst = [ar.alloc(F32, 128, D) for _ in range(2)]
            vstb = [self.buf("vst") for _ in range(2)]
            kbf = [ar.alloc(BF16, 128, D) for _ in range(2)]
            kbfb = [self.buf("kbf") for _ in range(2)]
            vcb = [ar.alloc(BF16, 128, D) for _ in range(2)]
            vcbb = [self.buf("vcb") for _ in range(2)]
            kTc = [ar.alloc(BF16, 128, 8, 128) for _ in range(2)]
            kTcb = [self.buf("kTc") for _ in range(2)]
            QS = ar.alloc(BF16, 128, 8, 32)
            QSb = self.buf("QS")
            KN = ar.alloc(BF16, 128, 8, 32)
            KNb = self.buf("KN")
            VN = ar.alloc(BF16, 16, 2, D)
            VNb = self.buf("VN")
            QS0 = ar.alloc(BF16, 128, 8, 32)
            QS0b = self.buf("QS0")
            self.dma(QS0, S["qs"][:, :, SCOL:SCOL + 32].rearrange("c p n -> p c n"), (self.dbuf["qs"],), (QS0b,))
            self.cp("dve", QS, QS0, (QS0b,), (QSb,))
            self.dma(KN, S["ks"][:, :, SCOL:SCOL + 32].rearrange("c p n -> p c n"), (self.dbuf["ks"],), (KNb,))
            if KS2 != -2:
                self.dma(VN, S["vs"][SCOL:SCOL + 32, :].rearrange("(s p) d -> p s d", p=16), (self.dbuf["vs"],), (VNb,))
            D8 = ar.alloc(F32, 128, 256)
            D8b = self.buf("D8")
            KNp = ar.alloc(BF16, 128, 8, 128)
            KNpb = self.buf("KNp")
            QBD = ar.alloc(BF16, 128, 8, 32)
            QBDb = self.buf("QBD")
            VNp = ar.alloc(BF16, 128, D)
            VNpb = self.buf("VNp")
            ones16 = ar.alloc(BF16, 128, 128)
            o16b = self.buf("ones16")
            self.memset("dve", ones16, 0.0, (o16b,))
            self.cp("dve", ones16[0:16, :], self.onesb[0:16, :], (self.cb,), (o16b,))
            for s in range(2):
                Ob = self.next_bank("acc")
                Sb_ = self.next_bank("acc")
                nkt = PAST // 128
                self.memset("dve", QBD.rearrange("p a b -> p (a b)"), 0.0, (QBDb,))
                self.cp("dve", QBD[0:64, :, 0:16], QS[0:64, :, s * 16:(s + 1) * 16], (QSb,), (QBDb,))
                self.cp("dve", QBD[64:128, :, 16:32], QS[64:128, :, s * 16:(s + 1) * 16], (QSb,), (QBDb,))
                self.memset("dve", KNp.rearrange("p a b -> p (a b)"), 0.0, (KNpb,))
                self.cp("dve", KNp[:, :, 0:16], KN[:, :, s * 16:(s + 1) * 16], (KNb,), (KNpb,))
                self.memset("dve", VNp, 0.0, (VNpb,))
                self.dma(VNp[0:16, :], S["vs"][SCOL + s * 16:SCOL + (s + 1) * 16, :], (self.dbuf["vs"],), (VNpb,))
                for kt in range(nkt + 1):
                    if KS2 < 0:
                        continue
                    if KS3 == 1 and kt == nkt:
                        continue
                    if KS3 == 2 and kt < nkt:
                        continue
                    pt = PT[pti % 4]
                    ptb = PTb[pti % 4]
                    pti += 1
                    b = self.next_bank("s")
                    if kt < nkt:
                        i2 = kt % 2
                        self.dma(kst[i2], I["cache_a_k"][idx, s, kt * 128:(kt + 1) * 128, :], (), (kstb[i2],))
                        self.dma(vst[i2], I["cache_a_v"][idx, s, kt * 128:(kt + 1) * 128, :], (), (vstb[i2],))
                        self.cp("act", kbf[i2], kst[i2], (kstb[i2],), (kbfb[i2],))
                        self.cp("dve", vcb[i2], vst[i2], (vstb[i2],), (vcbb[i2],))
                        if KS2 < 1:
                            continue
                        bt = self.next_bank("s")
                        btv = self.bank[bt].bitcast(BF16)
                        for j in range(8):
                            self.tr(btv[:, j * 128:(j + 1) * 128], kbf[i2][:, j * 128:(j + 1) * 128], self.identb,
                                    (kbfb[i2], self.cb), (self.bankb[bt],))
                        self.cp("act", kTc[i2].rearrange("p a b -> p (a b)"), btv, (self.bankb[bt],), (kTcb[i2],))
                        if KS2 < 2:
                            continue
                        nkeys = 128
                        for j in range(8):
                            self.mm(self.bank[b][:, j * 32:(j + 1) * 32], kTc[i2][:, j, :], QBD[:, j, :], True, True,
                                    (kTcb[i2], QBDb), (self.bankb[b],))
                        vsrc = lambda j: vcb[i2][:, j * 128:(j + 1) * 128]
                        vsb = vcbb[i2]
                    else:
                        nkeys = 128
                        for j in range(8):
                            self.mm(self.bank[b][:, j * 32:(j + 1) * 32], KNp[:, j, :], QBD[:, j, :], True, True,
                                    (KNpb, QBDb), (self.bankb[b],))
                        vsrc = lambda j: VNp[:, j * 128:(j + 1) * 128]
                        vsb = VNpb
                    if KS2 < 1 and kt == nkt:
                        continue
                    if KS2 < 3:
                        continue
                    self.act(pt[0:nkeys, 0:256], self.bank[b][0:nkeys, 0:256], AF.Exp, (self.bankb[b],), (ptb,), scale=0.125)
                    if KS2 < 4:
                        continue
                    for j in range(8):
                        for c in range(2):
                            col = (j * 2 + c) * 16
                            self.mm(self.bank[Ob][:, col:col + 16], vsrc(j), pt[0:nkeys, col:col + 16],
                                    (kt == 0 and j == 0 and c == 0), kt == nkt, (vsb, ptb), (self.bankb[Ob],))
                    self.mm(self.bank[Sb_][:, 0:256], (self.onesb if kt < nkt else ones16), pt[0:nkeys, 0:256], kt == 0, kt == nkt,
                            (ptb, self.cb, o16b), (self.bankb[Sb_],))
                if KS2 < 5:
                    continue
                r = ep["r0"][:, 0:256]
                self.act(r, self.bank[Sb_][:, 0:256], AF.Ln, (self.bankb[Sb_],), (epb["r0"],))
                self.act(r, r, AF.Exp, (epb["r0"],), (epb["r0"],), scale=-1.0)
                self.tt("dve", D8, self.bank[Ob][:, 0:256], r, ALU.mult, (self.bankb[Ob], epb["r0"]), (D8b,))
                D8v = D8.rearrange("p (j c q) -> p j c q", c=2, q=16)
                Av = ep["A"][:, 0:128].rearrange("p (j q) -> p j q", q=16)
                self.stt(Av, D8v[:, :, 1, :], neglam[:, 0:1], D8v[:, :, 0, :], ALU.mult, ALU.add, (D8b, vb), (epb["A"],))
                self.act(ep["sq"][:, 0:128], ep["A"][:, 0:128], AF.Square, (epb["A"],), (epb["sq"],))
                b = self.next_bank("s")
                self.mm(self.bank[b][:, 0:128], self.onesb, ep["sq"][:, 0:128], True, True, (epb["sq"], self.cb), (self.bankb[b],))
                self.rstd(ep["rs"][:, 0:128], self.bank[b][:, 0:128], 1.0 / 128, (self.bankb[b],), epb["rs"])
                self.stt(attnT[:, :, s * 16:(s + 1) * 16], Av, gsub[:, 0:1],
                         ep["rs"][:, 0:128].rearrange("p (j q) -> p j q", q=16), ALU.mult, ALU.mult,
                         (epb["A"], epb["rs"], vb), (atb,))
            outproj(tile, 32)


            stt_["pti"] = pti
        if SAMPLE_FIRST:
            if KSTOP >= 6:
                do_sample()
            if not int(os.environ.get("KNOPROMPT", "0")):
                do_prompts()
        else:
            do_prompts()
            if KSTOP >= 6:
                do_sample()

    def pass_attn_c(self, layer):
        I, O, S, ar = self.I, self.O, self.S, self.ar
        self.phase()
        self._pools = {"s": [0, 1, 2, 3], "acc": [4, 5, 6], "g": [7]}
        Wo = ar.alloc(BF16, 128, 8, D)
        stg = [ar.alloc(F32, 128, 2048) for _ in range(2)]
        stgb = [self.buf("stg") for _ in range(2)]
        mark = ar.off - 2 * 2048
        self.load_weight(I["c_w_out"][0], Wo, D, D, stg, stgb)
        vb = self.buf("vec")
        tab = I["c_rel_bias"][0]
        ext = S["ext"]
        eb = self.dbuf["ext"]
        self.dma(ext[:, 128:385], tab.rearrange("m h -> h m"), (), (eb,), slow=True)
        e2 = ar.alloc(F32, 16, 2)
        e2b = self.buf("e2")
        ebc = ar.alloc(F32, 16, 256)
        self.dma(e2[:, 0:1], tab[0:1, :].rearrange("m h -> h m"), (), (e2b,), slow=True)
        self.dma(e2[:, 1:2], tab[256:257, :].rearrange("m h -> h m"), (), (e2b,), slow=True)
        self.cp("dve", ebc[:, 0:128], e2[:, 0:1].to_broadcast([16, 128]), (e2b,), (e2b,))
        self.cp("dve", ebc[:, 128:256], e2[:, 1:2].to_broadcast([16, 128]), (e2b,), (e2b,))
        self.dma(ext[:, 0:128], ebc[:, 0:128], (e2b,), (eb,))
        self.dma(ext[:, 385:512], ebc[:, 128:255], (e2b,), (eb,))
        self.P.barrier()
        ar.reset(mark)
        B3 = ar.alloc(F32, 128, 16, 128)
        B4 = ar.alloc(F32, 128, 16, 128)
        bconst = ar.alloc(F32, 128, 16)
        Bb = self.buf("bias")
        ext_t = ext.tensor
        brev = [ar.alloc(F32, 128, 128) for _ in range(2)]
        brevb = [self.buf("brev") for _ in range(2)]
        bi = 0
        for h in range(16):
            for (Bt, base) in ((B3, 384), (B4, 256)):
                srcap = bass.AP(tensor=ext_t, offset=h * 512 + base - 127, ap=[[1, 128], [1, 128]])
                self.dma(brev[bi % 2], srcap, (eb,), (brevb[bi % 2],))
                b = self.next_bank("s")
                self.mm(self.bank[b][:, 0:128], self.jmat, brev[bi % 2], True, True, (brevb[bi % 2], self.cb), (self.bankb[b],))
                self.cp("dve", Bt[:, h, :], self.bank[b][:, 0:128], (self.bankb[b],), (Bb,))
                bi += 1
        self.dma(bconst, tab[256:257, :].partition_broadcast(128), (), (Bb,))
        self.memset("dve", B4[64:128, :, 0:64], NEG, (Bb,))

        QT = ar.alloc(BF16, 128, 8, 512)
        QTb = self.buf("QT")
        xT = ar.alloc(F32, 128, 8, 512)
        xb = self.buf("xT")
        attnT = ar.alloc(BF16, 128, 8, 512)
        atb = self.buf("attnT")
        PT = [ar.alloc(BF16, 128, 128) for _ in range(4)]
        PTb = [self.buf("PT") for _ in range(4)]
        tmp = [ar.alloc(F32, 128, 128) for _ in range(2)]
        tmpb = [self.buf("tmp") for _ in range(2)]
        rc = ar.alloc(F32, 128, 128)
        rcb = self.buf("rc")
        kv_mark = ar.off
        KT = ar.alloc(BF16, 128, 8, T)
        KTb = [self.buf("KT") for _ in range(4)]
        VP = ar.alloc(BF16, 128, 16, 16, 128)
        VPb = [self.buf("VP") for _ in range(4)]
        self.memset("dve", VP.rearrange("p a b c -> p (a b c)"), 0.0, tuple(VPb))
        onesp = [self.onespb[:, 0:128], self.onespb[:, 128:256]]
        o16 = ar.alloc(BF16, 128, 256)
        o16b = self.buf("o16")
        self.memset("dve", o16, 0.0, (o16b,))
        self.cp("dve", o16[0:16, :], self.onespb[0:16, :], (self.cb,), (o16b,))
        onesp16 = [o16[:, 0:128], o16[:, 128:256]]
        st = dict(pti=0, tmi=0)

        def outproj(tile, N):
            for co in range(8):
                b = self.next_bank("g")
                for j in range(8):
                    self.mm(self.bank[b][:, 0:N], Wo[:, j, co * 128:(co + 1) * 128], attnT[:, j, 0:N], j == 0, j == 7,
                            (self.wb, atb), (self.bankb[b],))
                self.tt("dve", xT[:, co, 0:N], self.bank[b][:, 0:N], xT[:, co, 0:N], ALU.add, (self.bankb[b], xb), (xb,))
            self.store_x(tile, xT, xb)

        def head_pair(i, keytiles, qsrc, qbuf, nq, dst):
            ab = self.next_bank("acc")
            first = True
            for hh in range(2):
                h = 2 * i + hh
                for (kfn, kbuf, vfn, vbuf, nkeys, mode, r) in keytiles:
                    b = self.next_bank("s")
                    self.mm(self.bank[b][0:nkeys, 0:nq], kfn(i, hh), qsrc(i, hh), True, True, (kbuf, qbuf), (self.bankb[b],))
                    pt = PT[st["pti"] % 4]
                    ptb = PTb[st["pti"] % 4]
                    st["pti"] += 1
                    if mode == "const":
                        self.act(pt[0:nkeys, 0:nq], self.bank[b][0:nkeys, 0:nq], AF.Exp, (self.bankb[b], Bb), (ptb,),
                                 scale=0.125, bias=bconst[0:nkeys, h:h + 1])
                    else:
                        tp = tmp[st["tmi"] % 2]
                        tpb = tmpb[st["tmi"] % 2]
                        st["tmi"] += 1
                        self.stt(tp[0:nkeys, 0:nq], self.bank[b][0:nkeys, 0:nq], 0.125, mode[0:nkeys, h, 0:nq], ALU.mult, ALU.add,
                                 (self.bankb[b], Bb), (tpb,))
                        self.act(pt[0:nkeys, 0:nq], tp[0:nkeys, 0:nq], AF.Exp, (tpb,), (ptb,))
                    if r == 0:
                        self.memset("dve", pt[0:64, 64:128], 0.0, (ptb,))
                    self.mm(self.bank[ab][:, 0:nq], vfn(h), pt[0:nkeys, 0:nq], first, False, (vbuf, ptb), (self.bankb[ab],))
                    osel = onesp16 if r == -2 else onesp
                    self.mm(self.bank[ab][:, 128:128 + nq], osel[hh][0:nkeys, :], pt[0:nkeys, 0:nq], False, False,
                            (ptb, self.cb, o16b), (self.bankb[ab],))
                    first = False
            self.act(rc[:, 0:nq], self.bank[ab][:, 128:128 + nq], AF.Ln, (self.bankb[ab],), (rcb,))
            self.act(rc[:, 0:nq], rc[:, 0:nq], AF.Exp, (rcb,), (rcb,), scale=-1.0)
            self.tt("dve", dst, self.bank[ab][:, 0:nq], rc[:, 0:nq], ALU.mult, (self.bankb[ab], rcb), (atb,))

        tiles = self.tiles()
        for s in range(2):
            for t in range(4):
                tile = tiles[s * 4 + t]
                col0 = tile[0]
                self.dma(KT[:, :, t * 512:(t + 1) * 512], S["ks"][:, :, col0:col0 + 512].rearrange("c p n -> p c n"),
                         (self.dbuf["ks"],), (KTb[t],))
                vsrc = S["vs"][col0:col0 + 512, :].rearrange("(u p) (h two e) -> p u h two e", p=128, two=2, e=64)
                for u in range(4):
                    for two in range(2):
                        self.dma(VP[:, 4 * t + u, two::2, two * 64:(two + 1) * 64], vsrc[:, u, :, two, :],
                                 (self.dbuf["vs"],), (VPb[t],))
                self.dma(QT, S["qs"][:, :, col0:col0 + 512].rearrange("c p n -> p c n"), (self.dbuf["qs"],), (QTb,))
                self.load_x(tile, xT, xb, False)
                for u in range(4):
                    qt = 4 * t + u
                    kts = []
                    for r in range(5):
                        kt = qt - 4 + r
                        if kt < 0:
                            continue
                        mode = "const" if r < 3 else (B3 if r == 3 else B4)
                        kts.append(((lambda i, hh, kt=kt: KT[hh * 64:(hh + 1) * 64, i, kt * 128:(kt + 1) * 128]), KTb[kt // 4],
                                    (lambda h, kt=kt: VP[:, kt, h, :]), VPb[kt // 4], 128, mode, r))
                    for i in range(8):
                        head_pair(i, kts, (lambda i, hh, u=u: QT[hh * 64:(hh + 1) * 64, i, u * 128:(u + 1) * 128]), QTb, 128,
                                  attnT[:, i, u * 128:(u + 1) * 128])
                outproj(tile, 512)
        self.P.barrier()
        ar.reset(kv_mark)
        tile = tiles[8]
        self.load_x(tile, xT, xb, False)
        kst = [ar.alloc(F32, 128, D) for _ in range(2)]
        kstb = [self.buf("kst") for _ in range(2)]
        vst = [ar.alloc(F32, 128, D) for _ in range(2)]
        vstb = [self.buf("vst") for _ in range(2)]
        kbf = [ar.alloc(BF16, 128, D) for _ in range(2)]
        kbfb = [self.buf("kbf") for _ in range(2)]
        KC = ar.alloc(BF16, 128, 8, 512)
        KCb = self.buf("KC")
        VC = ar.alloc(BF16, 128, 5, 16, 128)
        VCb = self.buf("VC")
        QS = ar.alloc(BF16, 128, 8, 32)
        QSb = self.buf("QS")
        KN = ar.alloc(BF16, 128, 8, 32)
        KNb = self.buf("KN")
        self.dma(QS, S["qs"][:, :, SCOL:SCOL + 32].rearrange("c p n -> p c n"), (self.dbuf["qs"],), (QSb,))
        self.dma(KN, S["ks"][:, :, SCOL:SCOL + 32].rearrange("c p n -> p c n"), (self.dbuf["ks"],), (KNb,))
        KNp = ar.alloc(BF16, 128, 8, 128)
        KNpb = self.buf("KNp")
        QBD = ar.alloc(BF16, 128, 8, 32)
        QBDb = self.buf("QBD")
        for s in range(2):
            self.memset("dve", VC.rearrange("p a b c -> p (a b c)"), 0.0, (VCb,))
            for kt in range(4):
                i2 = kt % 2
                self.dma(kst[i2], I["cache_c_k"][0, s, kt * 128:(kt + 1) * 128, :], (), (kstb[i2],))
                self.dma(vst[i2], I["cache_c_v"][0, s, kt * 128:(kt + 1) * 128, :], (), (vstb[i2],))
                self.cp("act", kbf[i2], kst[i2], (kstb[i2],), (kbfb[i2],))
                bt = self.next_bank("s")
                btv = self.bank[bt].bitcast(BF16)
                for j in range(8):
                    self.tr(btv[:, j * 128:(j + 1) * 128], kbf[i2][:, j * 128:(j + 1) * 128], self.identb,
                            (kbfb[i2], self.cb), (self.bankb[bt],))
                self.cp("act", KC[:, :, kt * 128:(kt + 1) * 128], btv.rearrange("p (a b) -> p a b", b=128), (self.bankb[bt],), (KCb,))
                vv = vst[i2].rearrange("p (h two e) -> p h two e", two=2, e=64)
                for two in range(2):
                    self.cp("dve", VC[:, kt, two::2, two * 64:(two + 1) * 64], vv[:, :, two, :], (vstb[i2],), (VCb,))
            vn = S["vs"][SCOL + s * 16:SCOL + (s + 1) * 16, :].rearrange("p (h two e) -> p h two e", two=2, e=64)
            for two in range(2):
                self.dma(VC[0:16, 4, two::2, two * 64:(two + 1) * 64], vn[:, :, two, :], (self.dbuf["vs"],), (VCb,))
            kts = []
            for kt in range(4):
                mode = "const" if kt < 3 else B3
                kts.append(((lambda i, hh, kt=kt: KC[:, i, kt * 128:(kt + 1) * 128]), KCb,
                            (lambda h, kt=kt: VC[:, kt, h, :]), VCb, 128, mode, -1))
            self.memset("dve", KNp.rearrange("p a b -> p (a b)"), 0.0, (KNpb,))
            self.cp("dve", KNp[:, :, 0:16], KN[:, :, s * 16:(s + 1) * 16], (KNb,), (KNpb,))
            self.memset("dve", QBD.rearrange("p a b -> p (a b)"), 0.0, (QBDb,))
            self.cp("dve", QBD[0:64, :, 0:16], QS[0:64, :, s * 16:(s + 1) * 16], (QSb,), (QBDb,))
            self.cp("dve", QBD[64:128, :, 16:32], QS[64:128, :, s * 16:(s + 1) * 16], (QSb,), (QBDb,))
            kts.append(((lambda i, hh: KNp[:, i, :]), KNpb,
                        (lambda h: VC[:, 4, h, :]), VCb, 128, B4, -2))
            for i in range(8):
                head_pair(i, kts, (lambda i, hh: QBD[:, i, hh * 16:(hh + 1) * 16]), QBDb, 16,
                          attnT[:, i, s * 16:(s + 1) * 16])
        outproj(tile, 32)

    def pass_b(self, layer):
        I, O, S, ar = self.I, self.O, self.S, self.ar
        self.phase()
        self._pools = {"g": [0, 1, 2, 3, 4, 5, 6, 7]}
        Wi = ar.alloc(BF16, 128, 8, 2 * D)
        Ga = ar.alloc(BF16, 128, 8, 256)
        Gx = ar.alloc(BF16, 128, 8, 256)
        Wo = ar.alloc(BF16, 128, 8, D)
        stg = [ar.alloc(F32, 128, 2048) for _ in range(2)]
        stgb = [self.buf("stg") for _ in range(2)]
        mark = ar.off - 2 * 2048
        self.load_weight(I["b_w_in"][0], Wi, D, 2 * D, stg, stgb)
        self.load_weight(I["b_gate_a_w"][0].rearrange("n c d -> (n c) d"), Ga, D, 256, stg, stgb)
        self.load_weight(I["b_gate_x_w"][0].rearrange("n c d -> (n c) d"), Gx, D, 256, stg, stgb)
        self.load_weight(I["b_w_out"][0], Wo, D, D, stg, stgb)
        self.P.barrier()
        ar.reset(mark)
        vb = self.buf("vec")

        def col(src1d):
            a = ar.alloc(F32, 128, 8)
            self.dma(a, src1d.rearrange("(c p) -> p c", p=128), (), (vb,), slow=True)
            return a
        bg = col(I["b_b_in"][0, 0:D])
        bu = col(I["b_b_in"][0, D:2 * D])
        cw = [col(I["b_conv_w"][0, j]) for j in range(4)]
        cbi = col(I["b_conv_b"][0])
        gab = col(I["b_gate_a_b"][0])
        gxb = col(I["b_gate_x_b"][0])
        lam = col(I["b_lambda"][0])
        onec = ar.alloc(F32, 128, 1)
        self.memset("dve", onec, 1.0, (vb,))
        m8 = ar.alloc(F32, 128, 8)
        m16 = ar.alloc(F32, 128, 8)
        self.act(m8, lam, AF.Exp, (vb,), (vb,), scale=-1.0)
        self.act(m8, m8, AF.Ln, (vb,), (vb,), bias=onec[:, 0:1])
        self.ts("dve", m16, m8, -16.0, None, ALU.mult, None, (vb,), (vb,))
        self.ts("dve", m8, m8, -8.0, None, ALU.mult, None, (vb,), (vb,))

        xT = ar.alloc(F32, 128, 8, 512)
        xb = self.buf("xT")
        hT = ar.alloc(BF16, 128, 8, 512)
        hb = self.buf("hT")
        sq = ar.alloc(BF16, 128, 8, 512)
        sqb = self.buf("sq")
        rs = ar.alloc(F32, 128, 512)
        rsb = self.buf("rs")
        gate = ar.alloc(BF16, 128, 8, 512)
        gateb = self.buf("gate")
        uext = ar.alloc(F32, 128, 8, 516)
        ub = self.buf("uext")
        hist = ar.alloc(F32, 128, 8, 3)
        histb = self.buf("hist")
        hstage = ar.alloc(F32, 128, 3, 8)
        hstb = self.buf("hstage")
        xc = ar.alloc(F32, 128, 8, 512)
        xcb = self.buf("xc")
        xcbf = ar.alloc(BF16, 128, 8, 512)
        xcbfb = self.buf("xcbf")
        hs = ar.alloc(F32, 128, 8, 512)
        hsb = self.buf("hs")
        hprev = ar.alloc(F32, 128, 8)
        hpb = self.buf("hprev")
        yin = ar.alloc(BF16, 128, 8, 512)
        yinb = self.buf("yin")
        sc = []
        for i in range(2):
            sc.append(dict(g1=ar.alloc(F32, 128, 512), g2=ar.alloc(F32, 128, 512), r=ar.alloc(F32, 128, 512),
                           ii=ar.alloc(F32, 128, 512), a=ar.alloc(F32, 128, 512), a2=ar.alloc(F32, 128, 512),
                           g1b=self.buf("g1"), g2b=self.buf("g2"), rb=self.buf("r"), iib=self.buf("ii"),
                           ab=self.buf("a"), a2b=self.buf("a2")))
        gmix = self.gmix[:, layer, :]
        tiles = [tl for tl in self.tiles() if tl[2] >= 0] + [(SCOL, 16, -1, 0), (SCOL + 16, 16, -2, 0)]
        ci = 0
        for tile in tiles:
            col0, N, s, t = tile
            self.load_x(tile, xT, xb, False)
            self.norm(xT, xb, N, gmix, hT, hb, sq, sqb, rs, rsb)
            if s >= 0:
                if t == 0:
                    self.memset("dve", uext[:, :, 0:3], 0.0, (ub,))
                else:
                    self.cp("act", uext[:, :, 0:3], hist, (histb,), (ub,))
            else:
                ss = -1 - s
                for tt_ in range(3):
                    self.dma(hstage[:, tt_, :], I["state_b_conv"][0, ss, tt_].rearrange("(c p) -> p c", p=128), (), (hstb,), slow=True)
                self.cp("act", uext[:, :, 0:3], hstage.rearrange("p t c -> p c t"), (hstb,), (ub,))
                self.dma(hprev, I["state_b_h"][0, ss].rearrange("(c p) -> p c", p=128), (), (hpb,), slow=True)
            for c in range(8):
                k = sc[ci % 2]
                ci += 1
                b = self.next_bank("g")
                for kc in range(8):
                    self.mm(self.bank[b][:, 0:N], Wi[:, kc, c * 128:(c + 1) * 128], hT[:, kc, 0:N], kc == 0, kc == 7,
                            (self.wb, hb), (self.bankb[b],))
                self.act(k["g1"][:, 0:N], self.bank[b][:, 0:N], AF.Identity, (self.bankb[b], vb), (k["g1b"],), bias=bg[:, c:c + 1])
                self.act(k["g2"][:, 0:N], self.bank[b][:, 0:N], AF.Square, (self.bankb[b], vb), (k["g2b"],), bias=bg[:, c:c + 1])
                self.ts("dve", k["g2"][:, 0:N], k["g2"][:, 0:N], 0.044715, 1.0, ALU.mult, ALU.add, (k["g2b"],), (k["g2b"],))
                self.tt("dve", k["g2"][:, 0:N], k["g2"][:, 0:N], k["g1"][:, 0:N], ALU.mult, (k["g2b"], k["g1b"]), (k["g2b"],))
                self.act(k["g2"][:, 0:N], k["g2"][:, 0:N], AF.Sigmoid, (k["g2b"],), (k["g2b"],), scale=1.5957691216057308)
                self.tt("dve", gate[:, c, 0:N], k["g1"][:, 0:N], k["g2"][:, 0:N], ALU.mult, (k["g1b"], k["g2b"]), (gateb,))
                b = self.next_bank("g")
                for kc in range(8):
                    self.mm(self.bank[b][:, 0:N], Wi[:, kc, D + c * 128:D + (c + 1) * 128], hT[:, kc, 0:N], kc == 0, kc == 7,
                            (self.wb, hb), (self.bankb[b],))
                self.act(uext[:, c, 3:3 + N], self.bank[b][:, 0:N], AF.Identity, (self.bankb[b], vb), (ub,), bias=bu[:, c:c + 1])
                self.ts("dve", xc[:, c, 0:N], uext[:, c, 0:N], cw[0][:, c:c + 1], cbi[:, c:c + 1], ALU.mult, ALU.add,
                        (ub, vb), (xcb,))
                for j in range(1, 4):
                    self.stt(xc[:, c, 0:N], uext[:, c, j:j + N], cw[j][:, c:c + 1], xc[:, c, 0:N], ALU.mult, ALU.add,
                             (ub, vb, xcb), (xcb,))
            self.cp("act", xcbf[:, :, 0:N], xc[:, :, 0:N], (xcb,), (xcbfb,))
            self.cp("act", hist, uext[:, :, N:N + 3], (ub,), (histb,))
            for c in range(8):
                k = sc[ci % 2]
                ci += 1
                n, hf = c // 2, c % 2
                b = self.next_bank("g")
                for kcl in range(2):
                    self.mm(self.bank[b][:, 0:N], Ga[:, 2 * n + kcl, hf * 128:(hf + 1) * 128], xcbf[:, 2 * n + kcl, 0:N],
                            kcl == 0, kcl == 1, (self.wb, xcbfb), (self.bankb[b],))
                self.act(k["r"][:, 0:N], self.bank[b][:, 0:N], AF.Sigmoid, (self.bankb[b], vb), (k["rb"],), bias=gab[:, c:c + 1])
                b = self.next_bank("g")
                for kcl in range(2):
                    self.mm(self.bank[b][:, 0:N], Gx[:, 2 * n + kcl, hf * 128:(hf + 1) * 128], xcbf[:, 2 * n + kcl, 0:N],
                            kcl == 0, kcl == 1, (self.wb, xcbfb), (self.bankb[b],))
                self.act(k["ii"][:, 0:N], self.bank[b][:, 0:N], AF.Sigmoid, (self.bankb[b], vb), (k["iib"],), bias=gxb[:, c:c + 1])
                self.act(k["a"][:, 0:N], k["r"][:, 0:N], AF.Exp, (k["rb"], vb), (k["ab"],), scale=m8[:, c:c + 1])
                self.act(k["a2"][:, 0:N], k["r"][:, 0:N], AF.Exp, (k["rb"], vb), (k["a2b"],), scale=m16[:, c:c + 1])
                self.ts("dve", k["a2"][:, 0:N], k["a2"][:, 0:N], -1.0, 1.0, ALU.mult, ALU.add, (k["a2b"],), (k["a2b"],))
                self.act(k["a2"][:, 0:N], k["a2"][:, 0:N], AF.Sqrt, (k["a2b"],), (k["a2b"],))
                self.tt("dve", k["ii"][:, 0:N], k["ii"][:, 0:N], xc[:, c, 0:N], ALU.mult, (k["iib"], xcb), (k["iib"],))
                self.tt("dve", k["ii"][:, 0:N], k["ii"][:, 0:N], k["a2"][:, 0:N], ALU.mult, (k["iib"], k["a2b"]), (k["iib"],))
                init = 0.0 if (s >= 0 and t == 0) else hprev[:, c:c + 1]
                aa, bbv, oo = k["a"][:, 0:N], k["ii"][:, 0:N], hs[:, c, 0:N]
                self.P.op("dve", (lambda e, aa=aa, bbv=bbv, oo=oo, init=init: e.tensor_tensor_scan(
                    out=oo, data0=aa, data1=bbv, initial=init, op0=ALU.mult, op1=ALU.add)),
                    (k["ab"], k["iib"], hpb), (hsb,))
                self.tt("dve", yin[:, c, 0:N], hs[:, c, 0:N], gate[:, c, 0:N], ALU.mult, (hsb, gateb), (yinb,))
            self.cp("act", hprev, hs[:, :, N - 1], (hsb,), (hpb,))
            for co in range(8):
                b = self.next_bank("g")
                for j in range(8):
                    self.mm(self.bank[b][:, 0:N], Wo[:, j, co * 128:(co + 1) * 128], yin[:, j, 0:N], j == 0, j == 7,
                            (self.wb, yinb), (self.bankb[b],))
                self.tt("dve", xT[:, co, 0:N], self.bank[b][:, 0:N], xT[:, co, 0:N], ALU.add, (self.bankb[b], xb), (xb,))
            self.store_x(tile, xT, xb)
            if s < 0 or t == 3:
                if s >= 0:
                    oc, oh = O["b_conv_p"][0, s], O["b_h_p"][0, s]
                else:
                    oc, oh = O["b_conv_s"][0, -1 - s], O["b_h_s"][0, -1 - s]
                self.cp("act", hstage.rearrange("p t c -> p c t"), hist, (histb,), (hstb,))
                for tt_ in range(3):
                    self.dma(oc[tt_].rearrange("(c p) -> p c", p=128), hstage[:, tt_, :], (hstb,), (), slow=True)
                self.dma(oh.rearrange("(c p) -> p c", p=128), hprev, (hpb,), (), slow=True)

    def pass_mlp(self, layer):
        I, O, S, ar = self.I, self.O, self.S, self.ar
        self.phase()
        self._pools = {"g": [0, 1, 2, 3], "y": [4, 5, 6, 7]}
        W1 = ar.alloc(BF16, 128, 8, 4 * D)
        W2 = ar.alloc(BF16, 128, 32, D)
        mark = ar.off
        stg = [ar.alloc(F32, 128, 2048) for _ in range(3)]
        stgb = [self.buf("stg") for _ in range(3)]
        self.load_weight(I["mlp_w1"][layer], W1, D, 4 * D, stg, stgb)
        self.load_weight(I["mlp_w2"][layer], W2, 4 * D, D, stg, stgb)
        self.P.barrier()
        ar.reset(mark)
        last = (layer == 3)
        nxb = 1 if last else 2
        xTl = [ar.alloc(F32, 128, 8, 512) for _ in range(nxb)]
        xbl = [self.buf("xT") for _ in range(nxb)]
        hT = ar.alloc(BF16, 128, 8, 512)
        hb = self.buf("hT")
        hid = ar.alloc(BF16, 128, 16, 512)
        hidb = [self.buf("hid") for _ in range(16)]
        sq = hid[:, 0:8, :]
        rs = ar.alloc(F32, 128, 512)
        rsb = self.buf("rs")
        rl = [ar.alloc(BF16, 128, 512) for _ in range(2)]
        rlb = [self.buf("rl") for _ in range(2)]
        if last:
            tm = ar.alloc(F32, 128, 4, D)
            tmb = self.buf("tm")
        gm = self.gmlp[:, layer, :]
        ri = 0
        hall = Buf("hid_all")
        tl_ = self.tiles()
        if nxb == 2:
            self.load_x(tl_[0], xTl[0], xbl[0], False)
        for ti_, tile in enumerate(tl_):
            col0, N, s, t = tile
            xT, xb = xTl[ti_ % nxb], xbl[ti_ % nxb]
            if nxb == 2:
                if ti_ + 1 < len(tl_):
                    self.load_x(tl_[ti_ + 1], xTl[(ti_ + 1) % 2], xbl[(ti_ + 1) % 2], False)
            else:
                self.load_x(tile, xT, xb, False)
            self.norm(xT, xb, N, gm, hT, hb, sq, hall, rs, rsb)
            for half in range(2):
                for fi in range(16):
                    f = half * 16 + fi
                    b = self.next_bank("g")
                    for kc in range(8):
                        self.mm(self.bank[b][:, 0:N], W1[:, kc, f * 128:(f + 1) * 128], hT[:, kc, 0:N], kc == 0, kc == 7,
                                (self.wb, hb), (self.bankb[b],))
                    r_, rb_ = rl[ri % 2], rlb[ri % 2]
                    ri += 1
                    self.act(r_[:, 0:N], self.bank[b][:, 0:N], AF.Relu, (self.bankb[b],), (rb_,))
                    self.tt("dve", hid[:, fi, 0:N], r_[:, 0:N], r_[:, 0:N], ALU.mult, (rb_,), (hall,))
                for co in range(8):
                    b = self.next_bank("y")
                    for fi in range(16):
                        f = half * 16 + fi
                        self.mm(self.bank[b][:, 0:N], W2[:, f, co * 128:(co + 1) * 128], hid[:, fi, 0:N], fi == 0, fi == 15,
                                (self.wb, hall), (self.bankb[b],))
                    self.tt("dve", xT[:, co, 0:N], self.bank[b][:, 0:N], xT[:, co, 0:N], ALU.add, (self.bankb[b], xb), (xb,))
            if not last:
                self.store_x(tile, xT, xb)
            else:
                self.norm(xT, xb, N, self.gfin, hT, hb, sq, hall, rs, rsb)
                for c in range(8):
                    self.stt(xT[:, c, 0:N], xT[:, c, 0:N], self.gfin[:, c:c + 1], rs[:, 0:N], ALU.mult, ALU.mult,
                             (xb, rsb, self.cb), (xb,))

                def rows(u, nt_):
                    if s >= 0:
                        r0 = t * 512 + u * 128
                        return [(O["y_prompt"][s, r0:r0 + 128, :], 0, 128)]
                    return [(O["y_sample"][0, :, :], 0, 16), (O["y_sample"][1, :, :], 16, 32)]
                self.out_tm(xT, xb, N, tm, tmb, rows)


_NC_CACHE = {}


def _consts():
    c = {}
    c["k_ident"] = np.eye(128, dtype=np.float32)
    c["k_jmat"] = np.ascontiguousarray(np.eye(128, dtype=np.float32)[::-1])
    c["k_ones"] = np.ones((128, 128), np.float32)
    bo = np.zeros((128, 128), np.float32)
    bo[0:64, 0:64] = 1.0
    bo[64:128, 64:128] = 1.0
    c["k_bones"] = bo
    rm = np.zeros((128, 128), np.float32)
    for p in range(128):
        d = p % 64
        if d < 32:
            rm[p + 32, p] = -1.0
        else:
            rm[p - 32, p] = 1.0
    c["k_rmat"] = rm
    op = np.zeros((128, 256), np.float32)
    op[:, 0:64] = 1.0
    op[:, 128 + 64:256] = 1.0
    c["k_onesp"] = op
    half = 32
    inv = (np.float32(10000.0) ** (-np.arange(half, dtype=np.float32) / np.float32(half))).astype(np.float32)
    pos = np.concatenate([np.arange(T), np.arange(T), PAST + np.arange(TS), PAST + np.arange(TS)]).astype(np.float32)
    ang = (pos[:, None] * inv[None, :]).astype(np.float32)
    cos = np.cos(ang).astype(np.float32)
    sin = np.sin(ang).astype(np.float32)
    fi = np.arange(128) % 32
    c["k_cos"] = np.ascontiguousarray(cos[:, fi].T)
    c["k_sin"] = np.ascontiguousarray(sin[:, fi].T)
    return c


def kernel(**inputs):
    if "nc" not in _NC_CACHE:
        _NC_CACHE["nc"] = K().build()
    nc = _NC_CACHE["nc"]
    consts = _consts()
    f = lambda a: np.ascontiguousarray(np.asarray(a, dtype=np.float32))
    in_maps = []
    shared = {}
    for nm in ["norm_mix_g", "norm_mlp_g", "norm_final_g", "a_w_in", "a_q_norm_g", "a_k_norm_g", "a_subln_g",
               "a_w_out", "b_w_in", "b_b_in", "b_conv_w", "b_conv_b", "b_gate_a_w", "b_gate_a_b", "b_gate_x_w",
               "b_gate_x_b", "b_lambda", "b_w_out", "c_w_in", "c_q_norm_g", "c_k_norm_g", "c_rel_bias", "c_w_out",
               "mlp_w1", "mlp_w2"]:
        shared[nm] = f(inputs[nm])
    shared["a_lambda"] = f(inputs["a_lambda"]).reshape(2, 256)
    shared.update(consts)
    for c in range(NCORES):
        m = dict(shared)
        sl = slice(2 * c, 2 * c + 2)
        m["x_prompt"] = f(inputs["x_prompt"][sl])
        m["x_sample"] = f(inputs["x_sample"][sl])
        m["cache_a_k"] = f(np.asarray(inputs["cache_a_k"])[:, sl].reshape(2, 2, PAST, D))
        m["cache_a_v"] = f(np.asarray(inputs["cache_a_v"])[:, sl].reshape(2, 2, PAST, D))
        m["state_b_conv"] = f(np.asarray(inputs["state_b_conv"])[:, sl])
        m["state_b_h"] = f(np.asarray(inputs["state_b_h"])[:, sl])
        m["cache_c_k"] = f(np.asarray(inputs["cache_c_k"])[:, sl].reshape(1, 2, 512, D))
        m["cache_c_v"] = f(np.asarray(inputs["cache_c_v"])[:, sl].reshape(1, 2, 512, D))
        in_maps.append(m)
    if KCORES < NCORES:
        res = run_bass_kernel_spmd(nc, in_maps[:KCORES], core_ids=list(range(KCORES)))
        R = list(res.results) + [res.results[0]] * (NCORES - KCORES)
    else:
        res = run_bass_kernel_spmd(nc, in_maps, core_ids=list(range(NCORES)))
        R = res.results
    if DEBUG:
        _NC_CACHE["res"] = R

    def cat(name, axis, shape):
        return np.concatenate([np.asarray(R[c][name]) for c in range(NCORES)], axis=axis).reshape(shape).astype(np.float32)
    B = 16
    return (
        cat("y_prompt", 0, (B, T, D)),
        cat("y_sample", 0, (B, TS, D)),
        cat("a_k_p", 1, (2, B, T, 8, 2, 64)),
        cat("a_v_p", 1, (2, B, T, 8, 128)),
        cat("a_k_s", 1, (2, B, TS, 8, 2, 64)),
        cat("a_v_s", 1, (2, B, TS, 8, 128)),
        cat("b_conv_p", 1, (1, B, 3, D)),
        cat("b_h_p", 1, (1, B, D)),
        cat("b_conv_s", 1, (1, B, 3, D)),
        cat("b_h_s", 1, (1, B, D)),
        cat("c_k_p", 1, (1, B, 512, 16, 64)),
        cat("c_v_p", 1, (1, B, 512, 16, 64)),
        cat("c_k_s", 1, (1, B, TS, 16, 64)),
        cat("c_v_s", 1, (1, B, TS, 16, 64)),
    )
```

```python
import math
from contextlib import ExitStack
import numpy as np
import concourse.bass as bass
import concourse.mybir as mybir
from concourse.bass_utils import run_bass_kernel_spmd

F32 = mybir.dt.float32
BF16 = mybir.dt.bfloat16
AF = mybir.ActivationFunctionType
ALU = mybir.AluOpType
AX = mybir.AxisListType

NCORES = 8
D = 1024
T = 2048
TS = 16
PAST = 4096
NTOK = 2 * T + 2 * TS
SCOL = 2 * T
EPS = 1e-6
NEG = -30000.0
import os
DEBUG = bool(int(os.environ.get("KDEBUG", "0")))
NLAYERS = int(os.environ.get("KLAYERS", "4"))
NPASS = int(os.environ.get("KPASS", "99"))
KCORES = int(os.environ.get("KCORES", "8"))
KTILES = os.environ.get("KTILES", "")
KSTOP = int(os.environ.get("KSTOP", "99"))
KS2 = int(os.environ.get("KS2", "99"))
KS3 = int(os.environ.get("KS3", "0"))
SAMPLE_FIRST = int(os.environ.get("KSF", "1"))
POOLENG = os.environ.get("KPOOL", "pool")


class Buf:
    __slots__ = ("name", "w", "r", "excl")

    def __init__(self, name, excl=False):
        self.name = name
        self.w = []
        self.r = []
        self.excl = excl


class Op:
    __slots__ = ("eng", "fn", "deps", "sig", "dma", "ndma", "sem", "val")


class Prog:
    ENG = ("pe", "act", "dve", "pool", "sp")
    NDS = 12

    def __init__(self):
        self.streams = {e: [] for e in self.ENG}
        self.dmaops = []
        self.pending = {e: [] for e in self.ENG}
        self.dfinal = [0] * self.NDS

    def op(self, eng, fn, reads=(), writes=(), dma=False, ndma=1):
        deps = []
        for b in reads:
            deps += b.w
            if b.excl:
                deps += [r for r in b.r if r.eng != eng]
        for b in writes:
            deps += b.w
            deps += b.r
        deps += self.pending[eng]
        self.pending[eng] = []
        o = Op()
        o.eng = eng
        o.fn = fn
        o.dma = dma
        o.ndma = ndma
        o.sig = dma
        o.sem = None
        o.val = 0
        if dma:
            if len(self.dmaops) >= self.NDS:
                deps.append(self.dmaops[-self.NDS])
            self.dmaops.append(o)
        dd = []
        seen = set()
        for d in deps:
            if id(d) in seen:
                continue
            seen.add(id(d))
            if d.eng == "pe" and eng == "pe" and (not d.dma) and (not dma):
                continue
            dd.append(d)
        o.deps = dd
        self.streams[eng].append(o)
        wset = set(id(b) for b in writes)
        for b in reads:
            if id(b) in wset:
                continue
            b.r = [r for r in b.r if (r.dma or r.eng != eng)] + [o]
        for b in writes:
            b.w = [o]
            b.r = []
        return o

    def barrier(self):
        deps = []
        for e in self.ENG:
            for o in reversed(self.streams[e]):
                if not o.dma:
                    deps.append(o)
                    break
        deps += self.dmaops[-self.NDS:]
        for e in self.ENG:
            self.pending[e] = self.pending[e] + deps

    def finalize(self):
        for e in self.ENG:
            for o in self.streams[e]:
                for d in o.deps:
                    d.sig = True
        for e in self.ENG:
            c = 0
            for o in self.streams[e]:
                if o.dma:
                    continue
                if o.sig:
                    c += 1
                    o.sem = ("E", e)
                    o.val = c
        for i, o in enumerate(self.dmaops):
            k = i % self.NDS
            self.dfinal[k] += 16 * o.ndma
            o.sem = ("D", k)
            o.val = self.dfinal[k]

    def emit(self, eng, e, sems):
        waited = {}
        for o in self.streams[eng]:
            for d in o.deps:
                if waited.get(d.sem, 0) < d.val:
                    e.wait_ge(sems[d.sem], d.val)
                    waited[d.sem] = d.val
            r = o.fn(e)
            if o.dma:
                insts = r if isinstance(r, (list, tuple)) else [r]
                assert len(insts) == o.ndma, (len(insts), o.ndma)
                for ins in insts:
                    ins.then_inc(sems[o.sem], 16)
            elif o.sig:
                r.then_inc(sems[o.sem], 1)
        if eng == "sp":
            for k in range(self.NDS):
                if self.dfinal[k] > 0:
                    e.wait_ge(sems[("D", k)], self.dfinal[k])


class Arena:
    def __init__(self, t, words):
        self.t = t
        self.words = words
        self.off = 0

    def reset(self, off=0):
        self.off = off

    def alloc(self, dtype, npart, *shape):
        n = 1
        for s in shape:
            n *= s
        w = n if dtype == F32 else (n + 1) // 2
        w = (w + 1) // 2 * 2
        assert self.off + w <= self.words, ("SBUF arena overflow", self.off, w, self.words)
        a = self.t[0:npart, self.off:self.off + w]
        self.off += w
        if dtype != F32:
            a = a.bitcast(dtype)
        a = a[:, 0:n]
        if len(shape) == 2:
            a = a.rearrange("p (a b) -> p a b", b=shape[1])
        elif len(shape) == 3:
            a = a.rearrange("p (a b c) -> p a b c", b=shape[1], c=shape[2])
        return a


class K:
    def __init__(self):
        self.nc = bass.Bass("TRN2", target_bir_lowering=False)
        self.P = Prog()
        self.bufid = 0

    def buf(self, name="b"):
        self.bufid += 1
        return Buf(f"{name}{self.bufid}")

    def mm(self, out, lhsT, rhs, start, stop, reads, writes):
        self.P.op("pe", lambda e: e.matmul(out, lhsT=lhsT, rhs=rhs, start=start, stop=stop,
                                           skip_group_check=True), reads, writes)

    def tr(self, out, in_, ident, reads, writes):
        self.P.op("pe", lambda e: e.transpose(out, in_, ident), reads, writes)

    def act(self, out, in_, func, reads, writes, scale=None, bias=None):
        kw = {}
        if scale is not None:
            kw["scale"] = scale
        if bias is not None:
            kw["bias"] = bias
        self.P.op("act", lambda e: e.activation(out=out, in_=in_, func=func, **kw), reads, writes)

    def tt(self, eng, out, in0, in1, op, reads, writes):
        self.P.op(eng, lambda e: e.tensor_tensor(out=out, in0=in0, in1=in1, op=op), reads, writes)

    def ts(self, eng, out, in0, s1, s2, op0, op1, reads, writes):
        if op1 is None:
            self.P.op(eng, lambda e: e.tensor_scalar(out=out, in0=in0, scalar1=s1, scalar2=None, op0=op0),
                      reads, writes)
        else:
            self.P.op(eng, lambda e: e.tensor_scalar(out=out, in0=in0, scalar1=s1, scalar2=s2, op0=op0, op1=op1),
                      reads, writes)

    def stt(self, out, in0, scalar, in1, op0, op1, reads, writes):
        self.P.op("dve", lambda e: e.scalar_tensor_tensor(out=out, in0=in0, scalar=scalar, in1=in1,
                                                          op0=op0, op1=op1), reads, writes)

    def cp(self, eng, out, in_, reads, writes):
        if eng == "act":
            self.P.op("act", lambda e: e.copy(out=out, in_=in_), reads, writes)
        else:
            self.P.op(eng, lambda e: e.tensor_copy(out=out, in_=in_), reads, writes)

    def memset(self, eng, ap, val, writes):
        self.P.op(eng, lambda e: e.memset(ap, val), (), writes)

    def dma(self, out, in_, reads, writes, slow=False):
        if slow:
            self.P.op("sp", lambda e: e.dma_start(out=out, in_=in_, allow_slow_non_contiguous=True),
                      reads, writes, dma=True)
        else:
            self.P.op("sp", lambda e: e.dma_start(out=out, in_=in_), reads, writes, dma=True)

    def build(self):
        nc = self.nc

        def din(name, shape, dt=F32):
            return nc.dram_tensor(name, list(shape), dt, kind="ExternalInput").ap()

        def dout(name, shape, dt=F32):
            return nc.dram_tensor(name, list(shape), dt, kind="ExternalOutput").ap()

        def dscr(name, shape, dt=F32):
            return nc.dram_tensor(name, list(shape), dt, kind="Internal").ap()

        I = {}
        I["x_prompt"] = din("x_prompt", (2, T, D))
        I["x_sample"] = din("x_sample", (2, TS, D))
        I["cache_a_k"] = din("cache_a_k", (2, 2, PAST, D))
        I["cache_a_v"] = din("cache_a_v", (2, 2, PAST, D))
        I["state_b_conv"] = din("state_b_conv", (1, 2, 3, D))
        I["state_b_h"] = din("state_b_h", (1, 2, D))
        I["cache_c_k"] = din("cache_c_k", (1, 2, 512, D))
        I["cache_c_v"] = din("cache_c_v", (1, 2, 512, D))
        for nm, shp in [("norm_mix_g", (4, D)), ("norm_mlp_g", (4, D)), ("norm_final_g", (D,)),
                        ("a_w_in", (2, D, 3 * D)), ("a_q_norm_g", (2, 64)), ("a_k_norm_g", (2, 64)),
                        ("a_lambda", (2, 256)), ("a_subln_g", (2, 128)), ("a_w_out", (2, D, D)),
                        ("b_w_in", (1, D, 2 * D)), ("b_b_in", (1, 2 * D)), ("b_conv_w", (1, 4, D)),
                        ("b_conv_b", (1, D)), ("b_gate_a_w", (1, 4, 256, 256)), ("b_gate_a_b", (1, D)),
                        ("b_gate_x_w", (1, 4, 256, 256)), ("b_gate_x_b", (1, D)), ("b_lambda", (1, D)),
                        ("b_w_out", (1, D, D)), ("c_w_in", (1, D, 3 * D)), ("c_q_norm_g", (1, 64)),
                        ("c_k_norm_g", (1, 64)), ("c_rel_bias", (1, 257, 16)), ("c_w_out", (1, D, D)),
                        ("mlp_w1", (4, D, 4 * D)), ("mlp_w2", (4, 4 * D, D)),
                        ("k_ident", (128, 128)), ("k_jmat", (128, 128)), ("k_ones", (128, 128)), ("k_bones", (128, 128)),
                        ("k_rmat", (128, 128)), ("k_onesp", (128, 256)),
                        ("k_cos", (128, 2 * T + 32)), ("k_sin", (128, 2 * T + 32))]:
            I[nm] = din(nm, shp)
        O = {}
        O["y_prompt"] = dout("y_prompt", (2, T, D))
        O["y_sample"] = dout("y_sample", (2, TS, D))
        O["a_k_p"] = dout("a_k_p", (2, 2, T, D))
        O["a_v_p"] = dout("a_v_p", (2, 2, T, D))
        O["a_k_s"] = dout("a_k_s", (2, 2, TS, D))
        O["a_v_s"] = dout("a_v_s", (2, 2, TS, D))
        O["b_conv_p"] = dout("b_conv_p", (1, 2, 3, D))
        O["b_h_p"] = dout("b_h_p", (1, 2, D))
        O["b_conv_s"] = dout("b_conv_s", (1, 2, 3, D))
        O["b_h_s"] = dout("b_h_s", (1, 2, D))
        O["c_k_p"] = dout("c_k_p", (1, 2, 512, D))
        O["c_v_p"] = dout("c_v_p", (1, 2, 512, D))
        O["c_k_s"] = dout("c_k_s", (1, 2, TS, D))
        O["c_v_s"] = dout("c_v_s", (1, 2, TS, D))
        self.I, self.O = I, O
        S = {}
        S["xs"] = (dout if DEBUG else dscr)("s_xs", (8, 128, NTOK))
        S["qs"] = dscr("s_qs", (8, 128, NTOK), BF16)
        S["ks"] = dscr("s_ks", (8, 128, NTOK), BF16)
        S["vs"] = dscr("s_vs", (NTOK, D), BF16)
        S["ext"] = dscr("s_ext", (16, 512))
        self.S = S
        self.dbuf = {k: Buf("dram_" + k) for k in list(S.keys())}

        with ExitStack() as es:
            AW = 50 * 1024
            arena_t = es.enter_context(nc.sbuf_tensor("arena", [128, AW], F32))
            ps_t = es.enter_context(nc.psum_tensor("ps", [128, 8, 512], F32))
            self.ar = Arena(arena_t, AW)
            self.bank = [ps_t[:, b, :] for b in range(8)]
            self.bankb = [Buf(f"bank{b}", excl=True) for b in range(8)]
            sems = {}
            for e in Prog.ENG:
                if e != "sp":
                    sems[("E", e)] = es.enter_context(nc.semaphore("s_" + e))
            for k in range(Prog.NDS):
                sems[("D", k)] = es.enter_context(nc.semaphore(f"s_d{k}"))

            self.record()
            self.P.finalize()
            P = self.P
            block = es.enter_context(nc.Block())

            @block.sync
            def _(e):
                P.emit("sp", e, sems)

            @block.tensor
            def _(e):
                P.emit("pe", e, sems)

            @block.scalar
            def _(e):
                P.emit("act", e, sems)

            @block.vector
            def _(e):
                P.emit("dve", e, sems)

            @block.gpsimd
            def _(e):
                P.emit("pool", e, sems)
        return nc

    def tiles(self):
        r = []
        for s in range(2):
            for t in range(4):
                r.append((s * T + t * 512, 512, s, t))
        r.append((SCOL, 32, -1, 0))
        if KTILES:
            r = [r[int(i)] for i in KTILES.split(",")]
        return r

    def load_consts(self):
        ar = self.ar
        I = self.I
        c = {}
        st = ar.alloc(F32, 128, 128)
        stb = self.buf("cst")
        self.ident = ar.alloc(F32, 128, 128)
        self.cb = self.buf("consts")
        self.dma(self.ident, I["k_ident"][:, :], (), (self.cb,))
        self.jmat = ar.alloc(F32, 128, 128)
        self.dma(self.jmat, I["k_jmat"][:, :], (), (self.cb,))
        self.identb = ar.alloc(BF16, 128, 128)
        self.onesb = ar.alloc(BF16, 128, 128)
        self.bonesb = ar.alloc(BF16, 128, 128)
        self.rmatb = ar.alloc(BF16, 128, 128)
        self.onespb = ar.alloc(BF16, 128, 256)
        self.cp("dve", self.identb, self.ident, (self.cb,), (self.cb,))
        for dst, nm in [(self.onesb, "k_ones"), (self.bonesb, "k_bones"), (self.rmatb, "k_rmat")]:
            self.dma(st, I[nm][:, :], (), (stb,))
            self.cp("dve", dst, st, (stb,), (self.cb,))
        st2 = ar.alloc(F32, 128, 256)
        self.dma(st2, I["k_onesp"][:, :], (), (stb,))
        self.cp("dve", self.onespb, st2, (stb,), (self.cb,))
        self.gmix = ar.alloc(F32, 128, 4, 8)
        self.gmlp = ar.alloc(F32, 128, 4, 8)
        self.gfin = ar.alloc(F32, 128, 8)
        for l in range(4):
            self.dma(self.gmix[:, l, :], I["norm_mix_g"][l].rearrange("(c p) -> p c", p=128), (), (self.cb,), slow=True)
            self.dma(self.gmlp[:, l, :], I["norm_mlp_g"][l].rearrange("(c p) -> p c", p=128), (), (self.cb,), slow=True)
        self.dma(self.gfin, I["norm_final_g"].rearrange("(c p) -> p c", p=128), (), (self.cb,), slow=True)
        self.persist = ar.off

    def load_weight(self, src2d, dst, Kdim, Ncols, stg, stgb):
        engs = ("pool", "dve", "act")
        i = getattr(self, "_wl_i", 0)
        piece = 2048
        for kc in range(Kdim // 128):
            for c0 in range(0, Ncols, piece):
                n = min(piece, Ncols - c0)
                s = stg[i % len(stg)]
                sb = stgb[i % len(stg)]
                self.dma(s[:, 0:n], src2d[kc * 128:(kc + 1) * 128, c0:c0 + n], (), (sb,))
                self.cp(engs[i % 3], dst[:, kc, c0:c0 + n], s[:, 0:n], (sb,), (self.wb,))
                i += 1
        self._wl_i = i

    def rstd(self, out, ss, inv_n, reads, wbuf):
        self.act(out, ss, AF.Ln, reads, (wbuf,), scale=inv_n, bias=self.epsc)
        self.act(out, out, AF.Exp, (wbuf,), (wbuf,), scale=-0.5)

    def next_bank(self, pool):
        i = self._bk.get(pool, 0)
        lst = self._pools[pool]
        self._bk[pool] = i + 1
        return lst[i % len(lst)]

    def norm(self, xT, xb, N, gcol, hT, hb, sq, sqb, rs, rsb):
        self.act(sq[:, :, 0:N], xT[:, :, 0:N], AF.Square, (xb,), (sqb,))
        b = self.next_bank("g")
        for c in range(8):
            self.mm(self.bank[b][:, 0:N], self.onesb, sq[:, c, 0:N], c == 0, c == 7, (sqb, self.cb), (self.bankb[b],))
        self.rstd(rs[:, 0:N], self.bank[b][:, 0:N], 1.0 / D, (self.bankb[b],), rsb)
        for c in range(8):
            self.stt(hT[:, c, 0:N], xT[:, c, 0:N], gcol[:, c:c + 1], rs[:, 0:N], ALU.mult, ALU.mult,
                     (xb, rsb, self.cb), (hb,))

    def load_x(self, tile, xT, xb, first, xtm=None, xtmb=None):
        col0, N, s, t = tile
        if not first:
            self.dma(xT[:, :, 0:N], self.S["xs"][:, :, col0:col0 + N].rearrange("c p n -> p c n"),
                     (self.dbuf["xs"],), (xb,))
            return
        if s >= 0:
            for u in range(4):
                self.dma(xtm[:, u, :], self.I["x_prompt"][s, t * 512 + u * 128: t * 512 + (u + 1) * 128, :], (), (xtmb,))
            nsub, nt = 4, 128
        else:
            self.dma(xtm[0:32, 0, :], self.I["x_sample"].rearrange("s t d -> (s t) d"), (), (xtmb,))
            nsub, nt = 1, 32
        for c in range(8):
            b = self.next_bank("g")
            for u in range(nsub):
                self.tr(self.bank[b][:, u * 128:u * 128 + nt], xtm[0:nt, u, c * 128:(c + 1) * 128],
                        self.ident[0:nt, 0:nt], (xtmb, self.cb), (self.bankb[b],))
            self.cp("act" if c % 2 else "dve", xT[:, c, 0:N], self.bank[b][:, 0:N], (self.bankb[b],), (xb,))
        self.dma(self.S["xs"][:, :, col0:col0 + N].rearrange("c p n -> p c n"), xT[:, :, 0:N], (xb,), (self.dbuf["xs"],))

    def store_x(self, tile, xT, xb):
        col0, N, s, t = tile
        self.dma(self.S["xs"][:, :, col0:col0 + N].rearrange("c p n -> p c n"), xT[:, :, 0:N], (xb,), (self.dbuf["xs"],))

    def out_tm(self, src, srcb, N, tm, tmb, dst_rows_fn):
        nsub = (N + 127) // 128
        nt = min(N, 128)
        for u in range(nsub):
            for half in range(2):
                b = self.next_bank("g")
                for i in range(4):
                    self.tr(self.bank[b][0:nt, i * 128:(i + 1) * 128], src[:, half * 4 + i, u * 128:u * 128 + nt],
                            self.ident, (srcb, self.cb), (self.bankb[b],))
                self.cp("act" if half else "dve", tm[0:nt, u, half * 512:(half + 1) * 512], self.bank[b][0:nt, :],
                        (self.bankb[b],), (tmb,))
        for u in range(nsub):
            for (dst, r0, r1) in dst_rows_fn(u, nt):
                self.dma(dst, tm[r0:r1, u, :], (tmb,), ())

    def record(self):
        self._bk = {}
        self._pools = {"g": [0, 1, 2, 3, 4, 5, 6, 7]}
        self.epsc = EPS
        self.ar.reset(0)
        self.load_consts()
        self.epscol = self.ar.alloc(F32, 128, 1)
        self.memset("dve", self.epscol, EPS, (self.cb,))
        self.epsc = self.epscol
        self.persist = self.ar.off
        np_ = [0]

        def go():
            np_[0] += 1
            return np_[0] <= NPASS
        for layer in range(NLAYERS):
            kind = layer % 3
            idx = layer // 3
            if kind == 0:
                if go():
                    self.pass_proj(layer, "a", idx)
                if go():
                    self.pass_attn_a(layer, idx)
            elif kind == 1:
                if go():
                    self.pass_b(layer)
            else:
                if go():
                    self.pass_proj(layer, "c", 0)
                if go():
                    self.pass_attn_c(layer)
            if go():
                self.pass_mlp(layer)

    def phase(self):
        self.P.barrier()
        self.ar.reset(self.persist)
        self.wb = self.buf("w")

    def pass_proj(self, layer, kind, idx):
        I, O, S, ar = self.I, self.O, self.S, self.ar
        self.phase()
        self._pools = {"g": [0, 1, 2, 3, 4, 5, 6, 7]}
        wsrc = I["a_w_in"][idx] if kind == "a" else I["c_w_in"][0]
        Wb = ar.alloc(BF16, 128, 8, 3 * D)
        stg = [ar.alloc(F32, 128, 2048) for _ in range(2)]
        stgb = [self.buf("stg") for _ in range(2)]
        mark = ar.off - 2 * 2048
        self.load_weight(wsrc, Wb, D, 3 * D, stg, stgb)
        self.P.barrier()
        ar.reset(mark)
        if KSTOP < 1:
            return
        vb = self.buf("vec")
        gq = ar.alloc(F32, 128, 1)
        gk = ar.alloc(F32, 128, 1)
        gqp = ar.alloc(F32, 128, 1)
        gkp = ar.alloc(F32, 128, 1)
        qg = (I["a_q_norm_g"][idx] if kind == "a" else I["c_q_norm_g"][0]).rearrange("(d o) -> d o", o=1)
        kg = (I["a_k_norm_g"][idx] if kind == "a" else I["c_k_norm_g"][0]).rearrange("(d o) -> d o", o=1)
        for h in range(2):
            self.dma(gq[h * 64:(h + 1) * 64, :], qg, (), (vb,))
            self.dma(gk[h * 64:(h + 1) * 64, :], kg, (), (vb,))
            if kind == "a":
                self.dma(gqp[h * 64:h * 64 + 32, :], qg[32:64, :], (), (vb,))
                self.dma(gqp[h * 64 + 32:h * 64 + 64, :], qg[0:32, :], (), (vb,))
                self.dma(gkp[h * 64:h * 64 + 32, :], kg[32:64, :], (), (vb,))
                self.dma(gkp[h * 64 + 32:h * 64 + 64, :], kg[0:32, :], (), (vb,))
        if kind == "a":
            cosT = ar.alloc(F32, 128, 512)
            sinT = ar.alloc(F32, 128, 512)
            csb = self.buf("cs")
        nxb = 1 if layer == 0 else 2
        xTs = [ar.alloc(F32, 128, 8, 512) for _ in range(nxb)]
        xbs = [self.buf("xT") for _ in range(nxb)]
        xtm = ar.alloc(F32, 128, 4, D) if layer == 0 else None
        xtmb = self.buf("xtm")
        hT = ar.alloc(BF16, 128, 8, 512)
        hb = self.buf("hT")
        sq = ar.alloc(BF16, 128, 8, 512)
        sqb = self.buf("sq")
        rs = ar.alloc(F32, 128, 512)
        rsb = self.buf("rs")
        kpost = ar.alloc(F32, 128, 8, 512)
        kpb = self.buf("kpost")
        tm = ar.alloc(F32, 128, 4, D)
        tmb = self.buf("tm")
        vtm = ar.alloc(F32, 128, 4, D)
        vtmb = self.buf("vtm")
        vbf = ar.alloc(BF16, 128, 4, D)
        vbfb = self.buf("vbf")
        qT = ar.alloc(BF16, 128, 8, 512)
        qTb = self.buf("qT")
        kT = ar.alloc(BF16, 128, 8, 512)
        kTb = self.buf("kT")
        scr = []
        for i in range(2):
            scr.append(dict(raw=ar.alloc(BF16, 128, 512), sq=ar.alloc(BF16, 128, 512), rs=ar.alloc(F32, 128, 512),
                            t1=ar.alloc(F32, 128, 512), t2=ar.alloc(F32, 128, 512),
                            rawb=self.buf("raw"), sqb=self.buf("sq"), rsb=self.buf("rs"), t1b=self.buf("t1"),
                            t2b=self.buf("t2")))
        gmix = self.gmix[:, layer, :]
        ci = 0
        tl_ = self.tiles()
        if nxb == 2:
            self.load_x(tl_[0], xTs[0], xbs[0], False)
        for ti, tile in enumerate(tl_):
            col0, N, s, t = tile
            xT, xb = xTs[ti % nxb], xbs[ti % nxb]
            if KSTOP < 2:
                return
            if nxb == 2:
                if ti + 1 < len(tl_):
                    self.load_x(tl_[ti + 1], xTs[(ti + 1) % 2], xbs[(ti + 1) % 2], False)
            else:
                self.load_x(tile, xT, xb, layer == 0, xtm, xtmb)
            if KSTOP < 3:
                continue
            if kind == "a":
                tc0 = col0 if s >= 0 else 2 * T
                self.dma(cosT[:, 0:N], I["k_cos"][:, tc0:tc0 + N], (), (csb,))
                self.dma(sinT[:, 0:N], I["k_sin"][:, tc0:tc0 + N], (), (csb,))
            self.norm(xT, xb, N, gmix, hT, hb, sq, sqb, rs, rsb)
            if KSTOP < 4:
                continue
            for which in range(2):
                for j in range(8):
                    sc = scr[ci % 2]
                    ci += 1
                    b = self.next_bank("g")
                    R = self.bank[b][:, 0:N]
                    Rb = self.bankb[b]
                    for kc in range(8):
                        self.mm(R, Wb[:, kc, which * D + j * 128: which * D + (j + 1) * 128], hT[:, kc, 0:N],
                                kc == 0, kc == 7, (self.wb, hb), (Rb,))
                    gcol = gq if which == 0 else gk
                    gpcol = gqp if which == 0 else gkp
                    self.act(sc["sq"][:, 0:N], R, AF.Square, (Rb,), (sc["sqb"],))
                    b2 = self.next_bank("g")
                    self.mm(self.bank[b2][:, 0:N], self.bonesb, sc["sq"][:, 0:N], True, True, (sc["sqb"], self.cb),
                            (self.bankb[b2],))
                    self.rstd(sc["rs"][:, 0:N], self.bank[b2][:, 0:N], 1.0 / 64, (self.bankb[b2],), sc["rsb"])
                    if which == 0:
                        dst_f = None
                        dst_b = qT[:, j, 0:N]
                        dstb_buf = qTb
                    else:
                        dst_f = kpost[:, j, 0:N]
                        dst_b = kT[:, j, 0:N]
                        dstb_buf = kTb
                    if kind == "a":
                        self.cp("act", sc["raw"][:, 0:N], R, (Rb,), (sc["rawb"],))
                        b3 = self.next_bank("g")
                        self.mm(self.bank[b3][:, 0:N], self.rmatb, sc["raw"][:, 0:N], True, True,
                                (sc["rawb"], self.cb), (self.bankb[b3],))
                        self.stt(sc["t1"][:, 0:N], R, gcol[:, 0:1], cosT[:, 0:N], ALU.mult, ALU.mult,
                                 (Rb, vb, csb), (sc["t1b"],))
                        self.stt(sc["t2"][:, 0:N], self.bank[b3][:, 0:N], gpcol[:, 0:1], sinT[:, 0:N], ALU.mult, ALU.mult,
                                 (self.bankb[b3], vb, csb), (sc["t2b"],))
                        self.tt("dve", sc["t1"][:, 0:N], sc["t1"][:, 0:N], sc["t2"][:, 0:N], ALU.add,
                                (sc["t1b"], sc["t2b"]), (sc["t1b"],))
                        if which == 0:
                            self.tt("dve", dst_b, sc["t1"][:, 0:N], sc["rs"][:, 0:N], ALU.mult,
                                    (sc["t1b"], sc["rsb"]), (dstb_buf,))
                        else:
                            self.tt("dve", dst_f, sc["t1"][:, 0:N], sc["rs"][:, 0:N], ALU.mult,
                                    (sc["t1b"], sc["rsb"]), (kpb,))
                            self.cp("act", dst_b, dst_f, (kpb,), (dstb_buf,))
                    else:
                        if which == 0:
                            self.stt(dst_b, R, gcol[:, 0:1], sc["rs"][:, 0:N], ALU.mult, ALU.mult,
                                     (Rb, vb, sc["rsb"]), (dstb_buf,))
                        else:
                            self.stt(dst_f, R, gcol[:, 0:1], sc["rs"][:, 0:N], ALU.mult, ALU.mult,
                                     (Rb, vb, sc["rsb"]), (kpb,))
                            self.cp("act", dst_b, dst_f, (kpb,), (dstb_buf,))
            if KSTOP < 5:
                continue
            self.dma(S["qs"][:, :, col0:col0 + N].rearrange("c p n -> p c n"), qT[:, :, 0:N], (qTb,), (self.dbuf["qs"],))
            self.dma(S["ks"][:, :, col0:col0 + N].rearrange("c p n -> p c n"), kT[:, :, 0:N], (kTb,), (self.dbuf["ks"],))
            nsub = (N + 127) // 128
            nt = min(N, 128)
            for u in range(nsub):
                for half in range(2):
                    b = self.next_bank("g")
                    for kc in range(8):
                        self.mm(self.bank[b][0:nt, :], hT[:, kc, u * 128:u * 128 + nt],
                                Wb[:, kc, 2 * D + half * 512: 2 * D + (half + 1) * 512], kc == 0, kc == 7,
                                (self.wb, hb), (self.bankb[b],))
                    self.cp("act", vtm[0:nt, u, half * 512:(half + 1) * 512], self.bank[b][0:nt, :],
                            (self.bankb[b],), (vtmb,))
                    self.cp("dve", vbf[0:nt, u, half * 512:(half + 1) * 512], self.bank[b][0:nt, :],
                            (self.bankb[b],), (vbfb,))
            for u in range(nsub):
                self.dma(S["vs"][col0 + u * 128: col0 + u * 128 + nt, :], vbf[0:nt, u, :], (vbfb,), (self.dbuf["vs"],))
            if KSTOP < 6:
                continue
            want = True
            if kind == "c" and s >= 0 and t != 3:
                want = False
            if want:
                if kind == "a":
                    okp, ovp, oks, ovs = O["a_k_p"][idx], O["a_v_p"][idx], O["a_k_s"][idx], O["a_v_s"][idx]
                else:
                    okp, ovp, oks, ovs = O["c_k_p"][0], O["c_v_p"][0], O["c_k_s"][0], O["c_v_s"][0]

                def rows(dp, ds):
                    def fn(u, nt_):
                        if s >= 0:
                            if kind == "a":
                                r0 = t * 512 + u * 128
                            else:
                                r0 = u * 128
                            return [(dp[s, r0:r0 + 128, :], 0, 128)]
                        return [(ds[0, :, :], 0, 16), (ds[1, :, :], 16, 32)]
                    return fn
                self.out_tm(kpost, kpb, N, tm, tmb, rows(okp, oks))
                for u in range(nsub):
                    for (dst, r0, r1) in rows(ovp, ovs)(u, nt):
                        self.dma(dst, vtm[r0:r1, u, :], (vtmb,), ())

    def pass_attn_a(self, layer, idx):
        I, O, S, ar = self.I, self.O, self.S, self.ar
        self.phase()
        self._pools = {"s": [0, 1, 2, 3], "acc": [4, 5, 6, 7], "g": [0, 1, 2, 3]}
        lam_init = 0.8 - 0.6 * math.exp(-0.3 * layer)
        Wo = ar.alloc(BF16, 128, 8, D)
        stg = [ar.alloc(F32, 128, 2048) for _ in range(2)]
        stgb = [self.buf("stg") for _ in range(2)]
        mark = ar.off - 2 * 2048
        self.load_weight(I["a_w_out"][idx], Wo, D, D, stg, stgb)
        self.P.barrier()
        ar.reset(mark)
        vb = self.buf("vec")
        lp = ar.alloc(F32, 128, 256)
        self.dma(lp, I["a_lambda"][idx].partition_broadcast(128), (), (vb,))
        pr = ar.alloc(F32, 128, 128)
        s12 = ar.alloc(F32, 128, 2)
        neglam = ar.alloc(F32, 128, 1)
        self.tt("dve", pr[:, 0:64], lp[:, 0:64], lp[:, 64:128], ALU.mult, (vb,), (vb,))
        self.tt("dve", pr[:, 64:128], lp[:, 128:192], lp[:, 192:256], ALU.mult, (vb,), (vb,))
        self.P.op("dve", lambda e: e.tensor_reduce(out=s12, in_=pr.rearrange("p (a b) -> p a b", b=64), axis=AX.X,
                                                  op=ALU.add), (vb,), (vb,))
        self.act(s12, s12, AF.Exp, (vb,), (vb,))
        self.tt("dve", neglam, s12[:, 1:2], s12[:, 0:1], ALU.subtract, (vb,), (vb,))
        self.ts("dve", neglam, neglam, -lam_init, None, ALU.add, None, (vb,), (vb,))
        gsub = ar.alloc(F32, 128, 1)
        self.dma(gsub, I["a_subln_g"][idx].rearrange("(d o) -> d o", o=1), (), (vb,))
        self.ts("dve", gsub, gsub, 1.0 - lam_init, None, ALU.mult, None, (vb,), (vb,))

        if KSTOP < 1:
            return
        KT = ar.alloc(BF16, 128, 8, T)
        KTb = [self.buf("KT") for _ in range(4)]
        V = ar.alloc(BF16, 128, 16, D)
        Vb_ = [self.buf("V") for _ in range(4)]
        QTl = [ar.alloc(BF16, 128, 8, 512) for _ in range(2)]
        QTbl = [self.buf("QT") for _ in range(2)]
        xT = ar.alloc(F32, 128, 8, 512)
        xb = self.buf("xT")
        attnT = ar.alloc(BF16, 128, 8, 512)
        atb = self.buf("attnT")
        PT = [ar.alloc(BF16, 128, 512) for _ in range(4)]
        PTb = [self.buf("PT") for _ in range(4)]
        ep = dict(r0=ar.alloc(F32, 128, 512), r1=ar.alloc(F32, 128, 512), A=ar.alloc(F32, 128, 512),
                  Bm=ar.alloc(F32, 128, 512), sq=ar.alloc(BF16, 128, 512), rs=ar.alloc(F32, 128, 512))
        epb = {k: self.buf("ep_" + k) for k in ep}

        def epilogue(N, O0, O0b, S0, S0b, O1, O1b, S1, S1b, dst, dstb):
            self.act(ep["r0"][:, 0:N], S0, AF.Ln, (S0b,), (epb["r0"],))
            self.act(ep["r0"][:, 0:N], ep["r0"][:, 0:N], AF.Exp, (epb["r0"],), (epb["r0"],), scale=-1.0)
            self.act(ep["r1"][:, 0:N], S1, AF.Ln, (S1b,), (epb["r1"],))
            self.act(ep["r1"][:, 0:N], ep["r1"][:, 0:N], AF.Exp, (epb["r1"],), (epb["r1"],), scale=-1.0)
            self.tt("dve", ep["A"][:, 0:N], O0, ep["r0"][:, 0:N], ALU.mult, (O0b, epb["r0"]), (epb["A"],))
            self.tt("dve", ep["Bm"][:, 0:N], O1, ep["r1"][:, 0:N], ALU.mult, (O1b, epb["r1"]), (epb["Bm"],))
            self.stt(ep["A"][:, 0:N], ep["Bm"][:, 0:N], neglam[:, 0:1], ep["A"][:, 0:N], ALU.mult, ALU.add,
                     (epb["Bm"], epb["A"], vb), (epb["A"],))
            self.act(ep["sq"][:, 0:N], ep["A"][:, 0:N], AF.Square, (epb["A"],), (epb["sq"],))
            b = self.next_bank("s")
            self.mm(self.bank[b][:, 0:N], self.onesb, ep["sq"][:, 0:N], True, True, (epb["sq"], self.cb), (self.bankb[b],))
            self.rstd(ep["rs"][:, 0:N], self.bank[b][:, 0:N], 1.0 / 128, (self.bankb[b],), epb["rs"])
            self.stt(dst, ep["A"][:, 0:N], gsub[:, 0:1], ep["rs"][:, 0:N], ALU.mult, ALU.mult,
                     (epb["A"], epb["rs"], vb), (dstb,))

        def outproj(tile, N):
            for co in range(8):
                b = self.next_bank("s")
                for j in range(8):
                    self.mm(self.bank[b][:, 0:N], Wo[:, j, co * 128:(co + 1) * 128], attnT[:, j, 0:N], j == 0, j == 7,
                            (self.wb, atb), (self.bankb[b],))
                self.tt("dve", xT[:, co, 0:N], self.bank[b][:, 0:N], xT[:, co, 0:N], ALU.add, (self.bankb[b], xb), (xb,))
            self.store_x(tile, xT, xb)

        tiles = self.tiles()
        stt_ = dict(pti=0)

        def do_prompts():
            pti = stt_["pti"]
            for s in range(2):
                for t in range(4):
                    tile = tiles[s * 4 + t]
                    col0, N = tile[0], 512

                    def kvq_loads(ix, kv=True, q=True):
                        t2 = ix % 4
                        c0 = tiles[ix][0]
                        if kv:
                          self.dma(KT[:, :, t2 * 512:(t2 + 1) * 512], S["ks"][:, :, c0:c0 + 512].rearrange("c p n -> p c n"),
                                 (self.dbuf["ks"],), (KTb[t2],))
                          self.dma(V[:, 4 * t2:4 * t2 + 4, :], S["vs"][c0:c0 + 512, :].rearrange("(u p) d -> p u d", p=128),
                                 (self.dbuf["vs"],), (Vb_[t2],))
                        if q:
                          self.dma(QTl[ix % 2], S["qs"][:, :, c0:c0 + 512].rearrange("c p n -> p c n"), (self.dbuf["qs"],),
                                 (QTbl[ix % 2],))
                    ix_ = s * 4 + t
                    if ix_ == 0:
                        kvq_loads(0)
                    if t == 0 and s > 0:
                        kvq_loads(ix_, kv=True, q=False)
                    self.load_x(tile, xT, xb, False)
                    if ix_ + 1 < 8:
                        kvq_loads(ix_ + 1, kv=(t < 3), q=True)
                    QT, QTb = QTl[ix_ % 2], QTbl[ix_ % 2]
                    if KSTOP < 2:
                        continue
                    nk = 4 * t + 4
                    for j in range(8):
                        acc = [self.next_bank("acc") for _ in range(4)]
                        for c in range(2):
                            Ob, Sb_ = acc[2 * c], acc[2 * c + 1]
                            pr0 = c * 64

                            def qk(kt):
                                i = kt - 4 * t
                                q0 = 128 * i if i > 0 else 0
                                b = self.next_bank("s")
                                self.mm(self.bank[b][:, q0:512], KT[pr0:pr0 + 64, j, kt * 128:(kt + 1) * 128],
                                        QT[pr0:pr0 + 64, j, q0:512], True, True, (KTb[kt // 4], QTb), (self.bankb[b],))
                                return b, q0
                            pend = [qk(0)]
                            if nk > 1:
                                pend.append(qk(1))
                            for kt in range(nk):
                                b, q0 = pend.pop(0)
                                pt = PT[pti % 4]
                                ptb = PTb[pti % 4]
                                pti += 1
                                self.act(pt[:, q0:512], self.bank[b][:, q0:512], AF.Exp, (self.bankb[b],), (ptb,), scale=0.125)
                                i = kt - 4 * t
                                if i >= 0:
                                    self.memset("dve", pt[64:128, 128 * i:128 * i + 64], 0.0, (ptb,))
                                if kt + 2 < nk:
                                    pend.append(qk(kt + 2))
                                self.mm(self.bank[Ob][:, q0:512], V[:, kt, j * 128:(j + 1) * 128], pt[:, q0:512],
                                        kt == 0, kt == nk - 1, (Vb_[kt // 4], ptb), (self.bankb[Ob],))
                                self.mm(self.bank[Sb_][:, q0:512], self.onesb, pt[:, q0:512],
                                        kt == 0, kt == nk - 1, (ptb, self.cb), (self.bankb[Sb_],))
                        if KSTOP < 4:
                            continue
                        epilogue(512, self.bank[acc[0]], self.bankb[acc[0]], self.bank[acc[1]], self.bankb[acc[1]],
                                 self.bank[acc[2]], self.bankb[acc[2]], self.bank[acc[3]], self.bankb[acc[3]],
                                 attnT[:, j, :], atb)
                    if KSTOP < 5:
                        continue
                    outproj(tile, 512)

            stt_["pti"] = pti

        def do_sample():
            pti = stt_["pti"]
            tile = tiles[8]
            self.load_x(tile, xT, xb, False)
            kst = [ar.alloc(F32, 128, D) for _ in range(2)]
            kstb = [self.buf("kst") for _ in range(2)]
            vst = [ar.alloc(F32, 128, D) for _ in range(2)]
            vstb = [self.buf("vst") for _ in range(2)]
            kbf = [ar.alloc(BF16, 128, D) for _ in range(2)]
            kbfb = [self.buf("kbf") for _ in range(2)]
            vcb = [ar.alloc(BF16, 128, D) for _ in range(2)]
            vcbb = [self.buf("vcb") for _ in range(2)]
            kTc = [ar.alloc(BF16, 128, 8, 128) for _ in range(2)]
            kTcb = [self.buf("kTc") for _ in range(2)]
            QS = ar.alloc(BF16, 128, 8, 32)
            QSb = self.buf("QS")
            KN = ar.alloc(BF16, 128, 8, 32)
            KNb = self.buf("KN")
            VN = ar.alloc(BF16, 16, 2, D)
            VNb = self.buf("VN")
            QS0 = ar.alloc(BF16, 128, 8, 32)
            QS0b = self.buf("QS0")
            self.dma(QS0, S["qs"][:, :, SCOL:SCOL + 32].rearrange("c p n -> p c n"), (self.dbuf["qs"],), (QS0b,))
            self.cp("dve", QS, QS0, (QS0b,), (QSb,))
            self.dma(KN, S["ks"][:, :, SCOL:SCOL + 32].rearrange("c p n -> p c n"), (self.dbuf["ks"],), (KNb,))
            if KS2 != -2:
                self.dma(VN, S["vs"][SCOL:SCOL + 32, :].rearrange("(s p) d -> p s d", p=16), (self.dbuf["vs"],), (VNb,))
            D8 = ar.alloc(F32, 128, 256)
            D8b = self.buf("D8")
            KNp = ar.alloc(BF16, 128, 8, 128)
            KNpb = self.buf("KNp")
            QBD = ar.alloc(BF16, 128, 8, 32)
            QBDb = self.buf("QBD")
            VNp = ar.alloc(BF16, 128, D)
            VNpb = self.buf("VNp")
            ones16 = ar.alloc(BF16, 128, 128)
            o16b = self.buf("ones16")
            self.memset("dve", ones16, 0.0, (o16b,))
            self.cp("dve", ones16[0:16, :], self.onesb[0:16, :], (self.cb,), (o16b,))
            for s in range(2):
                Ob = self.next_bank("acc")
                Sb_ = self.next_bank("acc")
                nkt = PAST // 128
                self.memset("dve", QBD.rearrange("p a b -> p (a b)"), 0.0, (QBDb,))
                self.cp("dve", QBD[0:64, :, 0:16], QS[0:64, :, s * 16:(s + 1) * 16], (QSb,), (QBDb,))
                self.cp("dve", QBD[64:128, :, 16:32], QS[64:128, :, s * 16:(s + 1) * 16], (QSb,), (QBDb,))
                self.memset("dve", KNp.rearrange("p a b -> p (a b)"), 0.0, (KNpb,))
                self.cp("dve", KNp[:, :, 0:16], KN[:, :, s * 16:(s + 1) * 16], (KNb,), (KNpb,))
                self.memset("dve", VNp, 0.0, (VNpb,))
                self.dma(VNp[0:16, :], S["vs"][SCOL + s * 16:SCOL + (s + 1) * 16, :], (self.dbuf["vs"],), (VNpb,))
                for kt in range(nkt + 1):
                    if KS2 < 0:
                        continue
                    if KS3 == 1 and kt == nkt:
                        continue
                    if KS3 == 2 and kt < nkt:
                        continue
                    pt = PT[pti % 4]
                    ptb = PTb[pti % 4]
                    pti += 1
                    b = self.next_bank("s")
                    if kt < nkt:
                        i2 = kt % 2
                        self.dma(kst[i2], I["cache_a_k"][idx, s, kt * 128:(kt + 1) * 128, :], (), (kstb[i2],))
                        self.dma(vst[i2], I["cache_a_v"][idx, s, kt * 128:(kt + 1) * 128, :], (), (vstb[i2],))
                        self.cp("act", kbf[i2], kst[i2], (kstb[i2],), (kbfb[i2],))
                        self.cp("dve", vcb[i2], vst[i2], (vstb[i2],), (vcbb[i2],))
                        if KS2 < 1:
                            continue
                        bt = self.next_bank("s")
                        btv = self.bank[bt].bitcast(BF16)
                        for j in range(8):
                            self.tr(btv[:, j * 128:(j + 1) * 128], kbf[i2][:, j * 128:(j + 1) * 128], self.identb,
                                    (kbfb[i2], self.cb), (self.bankb[bt],))
                        self.cp("act", kTc[i2].rearrange("p a b -> p (a b)"), btv, (self.bankb[bt],), (kTcb[i2],))
                        if KS2 < 2:
                            continue
                        nkeys = 128
                        for j in range(8):
                            self.mm(self.bank[b][:, j * 32:(j + 1) * 32], kTc[i2][:, j, :], QBD[:, j, :], True, True,
                                    (kTcb[i2], QBDb), (self.bankb[b],))
                        vsrc = lambda j: vcb[i2][:, j * 128:(j + 1) * 128]
                        vsb = vcbb[i2]
                    else:
                        nkeys = 128
                        for j in range(8):
                            self.mm(self.bank[b][:, j * 32:(j + 1) * 32], KNp[:, j, :], QBD[:, j, :], True, True,
                                    (KNpb, QBDb), (self.bankb[b],))
                        vsrc = lambda j: VNp[:, j * 128:(j + 1) * 128]
                        vsb = VNpb
                    if KS2 < 1 and kt == nkt:
                        continue
                    if KS2 < 3:
                        continue
                    self.act(pt[0:nkeys, 0:256], self.bank[b][0:nkeys, 0:256], AF.Exp, (self.bankb[b],), (ptb,), scale=0.125)
                    if KS2 < 4:
                        continue
                    for j in range(8):
                        for c in range(2):
                            col = (j * 2 + c) * 16
                            self.mm(self.bank[Ob][:, col:col + 16], vsrc(j), pt[0:nkeys, col:col + 16],
                                    (kt == 0 and j == 0 and c == 0), kt == nkt, (vsb, ptb), (self.bankb[Ob],))
                    self.mm(self.bank[Sb_][:, 0:256], (self.onesb if kt < nkt else ones16), pt[0:nkeys, 0:256], kt == 0, kt == nkt,
                            (ptb, self.cb, o16b), (self.bankb[Sb_],))
                if KS2 < 5:
                    continue
                r = ep["r0"][:, 0:256]
                self.act(r, self.bank[Sb_][:, 0:256], AF.Ln, (self.bankb[Sb_],), (epb["r0"],))
                self.act(r, r, AF.Exp, (epb["r0"],), (epb["r0"],), scale=-1.0)
                self.tt("dve", D8, self.bank[Ob][:, 0:256], r, ALU.mult, (self.bankb[Ob], epb["r0"]), (D8b,))
                D8v = D8.rearrange("p (j c q) -> p j c q", c=2, q=16)
                Av = ep["A"][:, 0:128].rearrange("p (j q) -> p j q", q=16)
                self.stt(Av, D8v[:, :, 1, :], neglam[:, 0:1], D8v[:, :, 0, :], ALU.mult, ALU.add, (D8b, vb), (epb["A"],))
                self.act(ep["sq"][:, 0:128], ep["A"][:, 0:128], AF.Square, (epb["A"],), (epb["sq"],))
                b = self.next_bank("s")
                self.mm(self.bank[b][:, 0:128], self.onesb, ep["sq"][:, 0:128], True, True, (epb["sq"], self.cb), (self.bankb[b],))
                self.rstd(ep["rs"][:, 0:128], self.bank[b][:, 0:128], 1.0 / 128, (self.bankb[b],), epb["rs"])
                self.stt(attnT[:, :, s * 16:(s + 1) * 16], Av, gsub[:, 0:1],
                         ep["rs"][:, 0:128].rearrange("p (j q) -> p j q", q=16), ALU.mult, ALU.mult,
                         (epb["A"], epb["rs"], vb), (atb,))
            outproj(tile, 32)


            stt_["pti"] = pti
        if SAMPLE_FIRST:
            if KSTOP >= 6:
                do_sample()
            if not int(os.environ.get("KNOPROMPT", "0")):
                do_prompts()
        else:
            do_prompts()
            if KSTOP >= 6:
                do_sample()

    def pass_attn_c(self, layer):
        I, O, S, ar = self.I, self.O, self.S, self.ar
        self.phase()
        self._pools = {"s": [0, 1, 2, 3], "acc": [4, 5, 6], "g": [7]}
        Wo = ar.alloc(BF16, 128, 8, D)
        stg = [ar.alloc(F32, 128, 2048) for _ in range(2)]
        stgb = [self.buf("stg") for _ in range(2)]
        mark = ar.off - 2 * 2048
        self.load_weight(I["c_w_out"][0], Wo, D, D, stg, stgb)
        vb = self.buf("vec")
        tab = I["c_rel_bias"][0]
        ext = S["ext"]
        eb = self.dbuf["ext"]
        self.dma(ext[:, 128:385], tab.rearrange("m h -> h m"), (), (eb,), slow=True)
        e2 = ar.alloc(F32, 16, 2)
        e2b = self.buf("e2")
        ebc = ar.alloc(F32, 16, 256)
        self.dma(e2[:, 0:1], tab[0:1, :].rearrange("m h -> h m"), (), (e2b,), slow=True)
        self.dma(e2[:, 1:2], tab[256:257, :].rearrange("m h -> h m"), (), (e2b,), slow=True)
        self.cp("dve", ebc[:, 0:128], e2[:, 0:1].to_broadcast([16, 128]), (e2b,), (e2b,))
        self.cp("dve", ebc[:, 128:256], e2[:, 1:2].to_broadcast([16, 128]), (e2b,), (e2b,))
        self.dma(ext[:, 0:128], ebc[:, 0:128], (e2b,), (eb,))
        self.dma(ext[:, 385:512], ebc[:, 128:255], (e2b,), (eb,))
        self.P.barrier()
        ar.reset(mark)
        B3 = ar.alloc(F32, 128, 16, 128)
        B4 = ar.alloc(F32, 128, 16, 128)
        bconst = ar.alloc(F32, 128, 16)
        Bb = self.buf("bias")
        ext_t = ext.tensor
        brev = [ar.alloc(F32, 128, 128) for _ in range(2)]
        brevb = [self.buf("brev") for _ in range(2)]
        bi = 0
        for h in range(16):
            for (Bt, base) in ((B3, 384), (B4, 256)):
                srcap = bass.AP(tensor=ext_t, offset=h * 512 + base - 127, ap=[[1, 128], [1, 128]])
                self.dma(brev[bi % 2], srcap, (eb,), (brevb[bi % 2],))
                b = self.next_bank("s")
                self.mm(self.bank[b][:, 0:128], self.jmat, brev[bi % 2], True, True, (brevb[bi % 2], self.cb), (self.bankb[b],))
                self.cp("dve", Bt[:, h, :], self.bank[b][:, 0:128], (self.bankb[b],), (Bb,))
                bi += 1
        self.dma(bconst, tab[256:257, :].partition_broadcast(128), (), (Bb,))
        self.memset("dve", B4[64:128, :, 0:64], NEG, (Bb,))

        QTl = [ar.alloc(BF16, 128, 8, 512) for _ in range(2)]
        QTbl = [self.buf("QT") for _ in range(2)]
        xT = ar.alloc(F32, 128, 8, 512)
        xb = self.buf("xT")
        attnT = ar.alloc(BF16, 128, 8, 512)
        atb = self.buf("attnT")
        PT = [ar.alloc(BF16, 128, 128) for _ in range(4)]
        PTb = [self.buf("PT") for _ in range(4)]
        tmp = [ar.alloc(F32, 128, 128) for _ in range(2)]
        tmpb = [self.buf("tmp") for _ in range(2)]
        rc = ar.alloc(F32, 128, 128)
        rcb = self.buf("rc")
        kv_mark = ar.off
        KT = ar.alloc(BF16, 128, 8, T)
        KTb = [self.buf("KT") for _ in range(4)]
        VP = ar.alloc(BF16, 128, 16, 16, 128)
        VPb = [self.buf("VP") for _ in range(4)]
        self.memset("dve", VP.rearrange("p a b c -> p (a b c)"), 0.0, tuple(VPb))
        onesp = [self.onespb[:, 0:128], self.onespb[:, 128:256]]
        o16 = ar.alloc(BF16, 128, 256)
        o16b = self.buf("o16")
        self.memset("dve", o16, 0.0, (o16b,))
        self.cp("dve", o16[0:16, :], self.onespb[0:16, :], (self.cb,), (o16b,))
        onesp16 = [o16[:, 0:128], o16[:, 128:256]]
        st = dict(pti=0, tmi=0)

        def outproj(tile, N):
            for co in range(8):
                b = self.next_bank("g")
                for j in range(8):
                    self.mm(self.bank[b][:, 0:N], Wo[:, j, co * 128:(co + 1) * 128], attnT[:, j, 0:N], j == 0, j == 7,
                            (self.wb, atb), (self.bankb[b],))
                self.tt("dve", xT[:, co, 0:N], self.bank[b][:, 0:N], xT[:, co, 0:N], ALU.add, (self.bankb[b], xb), (xb,))
            self.store_x(tile, xT, xb)

        def head_pair(i, keytiles, qsrc, qbuf, nq, dst):
            ab = self.next_bank("acc")
            first = True
            for hh in range(2):
                h = 2 * i + hh
                for (kfn, kbuf, vfn, vbuf, nkeys, mode, r) in keytiles:
                    b = self.next_bank("s")
                    self.mm(self.bank[b][0:nkeys, 0:nq], kfn(i, hh), qsrc(i, hh), True, True, (kbuf, qbuf), (self.bankb[b],))
                    pt = PT[st["pti"] % 4]
                    ptb = PTb[st["pti"] % 4]
                    st["pti"] += 1
                    if mode == "const":
                        self.act(pt[0:nkeys, 0:nq], self.bank[b][0:nkeys, 0:nq], AF.Exp, (self.bankb[b], Bb), (ptb,),
                                 scale=0.125, bias=bconst[0:nkeys, h:h + 1])
                    else:
                        tp = tmp[st["tmi"] % 2]
                        tpb = tmpb[st["tmi"] % 2]
                        st["tmi"] += 1
                        self.stt(tp[0:nkeys, 0:nq], self.bank[b][0:nkeys, 0:nq], 0.125, mode[0:nkeys, h, 0:nq], ALU.mult, ALU.add,
                                 (self.bankb[b], Bb), (tpb,))
                        self.act(pt[0:nkeys, 0:nq], tp[0:nkeys, 0:nq], AF.Exp, (tpb,), (ptb,))
                    if r == 0:
                        self.memset("dve", pt[0:64, 64:128], 0.0, (ptb,))
                    self.mm(self.bank[ab][:, 0:nq], vfn(h), pt[0:nkeys, 0:nq], first, False, (vbuf, ptb), (self.bankb[ab],))
                    osel = onesp16 if r == -2 else onesp
                    self.mm(self.bank[ab][:, 128:128 + nq], osel[hh][0:nkeys, :], pt[0:nkeys, 0:nq], False, False,
                            (ptb, self.cb, o16b), (self.bankb[ab],))
                    first = False
            self.act(rc[:, 0:nq], self.bank[ab][:, 128:128 + nq], AF.Ln, (self.bankb[ab],), (rcb,))
            self.act(rc[:, 0:nq], rc[:, 0:nq], AF.Exp, (rcb,), (rcb,), scale=-1.0)
            self.tt("dve", dst, self.bank[ab][:, 0:nq], rc[:, 0:nq], ALU.mult, (self.bankb[ab], rcb), (atb,))

        tiles = self.tiles()
        for s in range(2):
            for t in range(4):
                tile = tiles[s * 4 + t]
                col0 = tile[0]

                def kvq_loads(ix, kv=True, q=True):
                    t2 = ix % 4
                    c0 = tiles[ix][0]
                    if kv:
                        self.dma(KT[:, :, t2 * 512:(t2 + 1) * 512], S["ks"][:, :, c0:c0 + 512].rearrange("c p n -> p c n"),
                                 (self.dbuf["ks"],), (KTb[t2],))
                        vsrc = S["vs"][c0:c0 + 512, :].rearrange("(u p) (h two e) -> p u h two e", p=128, two=2, e=64)
                        for u in range(4):
                            for two in range(2):
                                self.dma(VP[:, 4 * t2 + u, two::2, two * 64:(two + 1) * 64], vsrc[:, u, :, two, :],
                                         (self.dbuf["vs"],), (VPb[t2],))
                    if q:
                        self.dma(QTl[ix % 2], S["qs"][:, :, c0:c0 + 512].rearrange("c p n -> p c n"), (self.dbuf["qs"],),
                                 (QTbl[ix % 2],))
                ix_ = s * 4 + t
                if ix_ == 0:
                    kvq_loads(0)
                if t == 0 and s > 0:
                    kvq_loads(ix_, kv=True, q=False)
                self.load_x(tile, xT, xb, False)
                if ix_ + 1 < 8:
                    kvq_loads(ix_ + 1, kv=(t < 3), q=True)
                QT, QTb = QTl[ix_ % 2], QTbl[ix_ % 2]
                for u in range(4):
                    qt = 4 * t + u
                    kts = []
                    for r in range(5):
                        kt = qt - 4 + r
                        if kt < 0:
                            continue
                        mode = "const" if r < 3 else (B3 if r == 3 else B4)
                        kts.append(((lambda i, hh, kt=kt: KT[hh * 64:(hh + 1) * 64, i, kt * 128:(kt + 1) * 128]), KTb[kt // 4],
                                    (lambda h, kt=kt: VP[:, kt, h, :]), VPb[kt // 4], 128, mode, r))
                    for i in range(8):
                        head_pair(i, kts, (lambda i, hh, u=u: QT[hh * 64:(hh + 1) * 64, i, u * 128:(u + 1) * 128]), QTb, 128,
                                  attnT[:, i, u * 128:(u + 1) * 128])
                outproj(tile, 512)
        self.P.barrier()
        ar.reset(kv_mark)
        tile = tiles[8]
        self.load_x(tile, xT, xb, False)
        kst = [ar.alloc(F32, 128, D) for _ in range(2)]
        kstb = [self.buf("kst") for _ in range(2)]
        vst = [ar.alloc(F32, 128, D) for _ in range(2)]
        vstb = [self.buf("vst") for _ in range(2)]
        kbf = [ar.alloc(BF16, 128, D) for _ in range(2)]
        kbfb = [self.buf("kbf") for _ in range(2)]
        KC = ar.alloc(BF16, 128, 8, 512)
        KCb = self.buf("KC")
        VC = ar.alloc(BF16, 128, 5, 16, 128)
        VCb = self.buf("VC")
        QS = ar.alloc(BF16, 128, 8, 32)
        QSb = self.buf("QS")
        KN = ar.alloc(BF16, 128, 8, 32)
        KNb = self.buf("KN")
        self.dma(QS, S["qs"][:, :, SCOL:SCOL + 32].rearrange("c p n -> p c n"), (self.dbuf["qs"],), (QSb,))
        self.dma(KN, S["ks"][:, :, SCOL:SCOL + 32].rearrange("c p n -> p c n"), (self.dbuf["ks"],), (KNb,))
        KNp = ar.alloc(BF16, 128, 8, 128)
        KNpb = self.buf("KNp")
        QBD = ar.alloc(BF16, 128, 8, 32)
        QBDb = self.buf("QBD")
        for s in range(2):
            self.memset("dve", VC.rearrange("p a b c -> p (a b c)"), 0.0, (VCb,))
            for kt in range(4):
                i2 = kt % 2
                self.dma(kst[i2], I["cache_c_k"][0, s, kt * 128:(kt + 1) * 128, :], (), (kstb[i2],))
                self.dma(vst[i2], I["cache_c_v"][0, s, kt * 128:(kt + 1) * 128, :], (), (vstb[i2],))
                self.cp("act", kbf[i2], kst[i2], (kstb[i2],), (kbfb[i2],))
                bt = self.next_bank("s")
                btv = self.bank[bt].bitcast(BF16)
                for j in range(8):
                    self.tr(btv[:, j * 128:(j + 1) * 128], kbf[i2][:, j * 128:(j + 1) * 128], self.identb,
                            (kbfb[i2], self.cb), (self.bankb[bt],))
                self.cp("act", KC[:, :, kt * 128:(kt + 1) * 128], btv.rearrange("p (a b) -> p a b", b=128), (self.bankb[bt],), (KCb,))
                vv = vst[i2].rearrange("p (h two e) -> p h two e", two=2, e=64)
                for two in range(2):
                    self.cp("dve", VC[:, kt, two::2, two * 64:(two + 1) * 64], vv[:, :, two, :], (vstb[i2],), (VCb,))
            vn = S["vs"][SCOL + s * 16:SCOL + (s + 1) * 16, :].rearrange("p (h two e) -> p h two e", two=2, e=64)
            for two in range(2):
                self.dma(VC[0:16, 4, two::2, two * 64:(two + 1) * 64], vn[:, :, two, :], (self.dbuf["vs"],), (VCb,))
            kts = []
            for kt in range(4):
                mode = "const" if kt < 3 else B3
                kts.append(((lambda i, hh, kt=kt: KC[:, i, kt * 128:(kt + 1) * 128]), KCb,
                            (lambda h, kt=kt: VC[:, kt, h, :]), VCb, 128, mode, -1))
            self.memset("dve", KNp.rearrange("p a b -> p (a b)"), 0.0, (KNpb,))
            self.cp("dve", KNp[:, :, 0:16], KN[:, :, s * 16:(s + 1) * 16], (KNb,), (KNpb,))
            self.memset("dve", QBD.rearrange("p a b -> p (a b)"), 0.0, (QBDb,))
            self.cp("dve", QBD[0:64, :, 0:16], QS[0:64, :, s * 16:(s + 1) * 16], (QSb,), (QBDb,))
            self.cp("dve", QBD[64:128, :, 16:32], QS[64:128, :, s * 16:(s + 1) * 16], (QSb,), (QBDb,))
            kts.append(((lambda i, hh: KNp[:, i, :]), KNpb,
                        (lambda h: VC[:, 4, h, :]), VCb, 128, B4, -2))
            for i in range(8):
                head_pair(i, kts, (lambda i, hh: QBD[:, i, hh * 16:(hh + 1) * 16]), QBDb, 16,
                          attnT[:, i, s * 16:(s + 1) * 16])
        outproj(tile, 32)

    def pass_b(self, layer):
        I, O, S, ar = self.I, self.O, self.S, self.ar
        self.phase()
        self._pools = {"g": [0, 1, 2, 3, 4, 5, 6, 7]}
        Wi = ar.alloc(BF16, 128, 8, 2 * D)
        Ga = ar.alloc(BF16, 128, 8, 256)
        Gx = ar.alloc(BF16, 128, 8, 256)
        Wo = ar.alloc(BF16, 128, 8, D)
        stg = [ar.alloc(F32, 128, 2048) for _ in range(2)]
        stgb = [self.buf("stg") for _ in range(2)]
        mark = ar.off - 2 * 2048
        self.load_weight(I["b_w_in"][0], Wi, D, 2 * D, stg, stgb)
        self.load_weight(I["b_gate_a_w"][0].rearrange("n c d -> (n c) d"), Ga, D, 256, stg, stgb)
        self.load_weight(I["b_gate_x_w"][0].rearrange("n c d -> (n c) d"), Gx, D, 256, stg, stgb)
        self.load_weight(I["b_w_out"][0], Wo, D, D, stg, stgb)
        self.P.barrier()
        ar.reset(mark)
        vb = self.buf("vec")

        def col(src1d):
            a = ar.alloc(F32, 128, 8)
            self.dma(a, src1d.rearrange("(c p) -> p c", p=128), (), (vb,), slow=True)
            return a
        bg = col(I["b_b_in"][0, 0:D])
        bu = col(I["b_b_in"][0, D:2 * D])
        cw = [col(I["b_conv_w"][0, j]) for j in range(4)]
        cbi = col(I["b_conv_b"][0])
        gab = col(I["b_gate_a_b"][0])
        gxb = col(I["b_gate_x_b"][0])
        lam = col(I["b_lambda"][0])
        onec = ar.alloc(F32, 128, 1)
        self.memset("dve", onec, 1.0, (vb,))
        m8 = ar.alloc(F32, 128, 8)
        m16 = ar.alloc(F32, 128, 8)
        self.act(m8, lam, AF.Exp, (vb,), (vb,), scale=-1.0)
        self.act(m8, m8, AF.Ln, (vb,), (vb,), bias=onec[:, 0:1])
        self.ts("dve", m16, m8, -16.0, None, ALU.mult, None, (vb,), (vb,))
        self.ts("dve", m8, m8, -8.0, None, ALU.mult, None, (vb,), (vb,))

        xT = ar.alloc(F32, 128, 8, 512)
        xb = self.buf("xT")
        hT = ar.alloc(BF16, 128, 8, 512)
        hb = self.buf("hT")
        sq = ar.alloc(BF16, 128, 8, 512)
        sqb = self.buf("sq")
        rs = ar.alloc(F32, 128, 512)
        rsb = self.buf("rs")
        gate = ar.alloc(BF16, 128, 8, 512)
        gateb = self.buf("gate")
        uext = ar.alloc(F32, 128, 8, 516)
        ub = self.buf("uext")
        hist = ar.alloc(F32, 128, 8, 3)
        histb = self.buf("hist")
        hstage = ar.alloc(F32, 128, 3, 8)
        hstb = self.buf("hstage")
        xc = ar.alloc(F32, 128, 8, 512)
        xcb = self.buf("xc")
        xcbf = ar.alloc(BF16, 128, 8, 512)
        xcbfb = self.buf("xcbf")
        hs = ar.alloc(F32, 128, 8, 512)
        hsb = self.buf("hs")
        hprev = ar.alloc(F32, 128, 8)
        hpb = self.buf("hprev")
        yin = ar.alloc(BF16, 128, 8, 512)
        yinb = self.buf("yin")
        sc = []
        for i in range(2):
            sc.append(dict(g1=ar.alloc(F32, 128, 512), g2=ar.alloc(F32, 128, 512), r=ar.alloc(F32, 128, 512),
                           ii=ar.alloc(F32, 128, 512), a=ar.alloc(F32, 128, 512), a2=ar.alloc(F32, 128, 512),
                           g1b=self.buf("g1"), g2b=self.buf("g2"), rb=self.buf("r"), iib=self.buf("ii"),
                           ab=self.buf("a"), a2b=self.buf("a2")))
        gmix = self.gmix[:, layer, :]
        tiles = [tl for tl in self.tiles() if tl[2] >= 0] + [(SCOL, 16, -1, 0), (SCOL + 16, 16, -2, 0)]
        ci = 0
        for tile in tiles:
            col0, N, s, t = tile
            self.load_x(tile, xT, xb, False)
            self.norm(xT, xb, N, gmix, hT, hb, sq, sqb, rs, rsb)
            if s >= 0:
                if t == 0:
                    self.memset("dve", uext[:, :, 0:3], 0.0, (ub,))
                else:
                    self.cp("act", uext[:, :, 0:3], hist, (histb,), (ub,))
            else:
                ss = -1 - s
                for tt_ in range(3):
                    self.dma(hstage[:, tt_, :], I["state_b_conv"][0, ss, tt_].rearrange("(c p) -> p c", p=128), (), (hstb,), slow=True)
                self.cp("act", uext[:, :, 0:3], hstage.rearrange("p t c -> p c t"), (hstb,), (ub,))
                self.dma(hprev, I["state_b_h"][0, ss].rearrange("(c p) -> p c", p=128), (), (hpb,), slow=True)
            for c in range(8):
                k = sc[ci % 2]
                ci += 1
                b = self.next_bank("g")
                for kc in range(8):
                    self.mm(self.bank[b][:, 0:N], Wi[:, kc, c * 128:(c + 1) * 128], hT[:, kc, 0:N], kc == 0, kc == 7,
                            (self.wb, hb), (self.bankb[b],))
                self.act(k["g1"][:, 0:N], self.bank[b][:, 0:N], AF.Identity, (self.bankb[b], vb), (k["g1b"],), bias=bg[:, c:c + 1])
                self.act(k["g2"][:, 0:N], self.bank[b][:, 0:N], AF.Square, (self.bankb[b], vb), (k["g2b"],), bias=bg[:, c:c + 1])
                self.ts("dve", k["g2"][:, 0:N], k["g2"][:, 0:N], 0.044715, 1.0, ALU.mult, ALU.add, (k["g2b"],), (k["g2b"],))
                self.tt("dve", k["g2"][:, 0:N], k["g2"][:, 0:N], k["g1"][:, 0:N], ALU.mult, (k["g2b"], k["g1b"]), (k["g2b"],))
                self.act(k["g2"][:, 0:N], k["g2"][:, 0:N], AF.Sigmoid, (k["g2b"],), (k["g2b"],), scale=1.5957691216057308)
                self.tt("dve", gate[:, c, 0:N], k["g1"][:, 0:N], k["g2"][:, 0:N], ALU.mult, (k["g1b"], k["g2b"]), (gateb,))
                b = self.next_bank("g")
                for kc in range(8):
                    self.mm(self.bank[b][:, 0:N], Wi[:, kc, D + c * 128:D + (c + 1) * 128], hT[:, kc, 0:N], kc == 0, kc == 7,
                            (self.wb, hb), (self.bankb[b],))
                self.act(uext[:, c, 3:3 + N], self.bank[b][:, 0:N], AF.Identity, (self.bankb[b], vb), (ub,), bias=bu[:, c:c + 1])
                self.ts("dve", xc[:, c, 0:N], uext[:, c, 0:N], cw[0][:, c:c + 1], cbi[:, c:c + 1], ALU.mult, ALU.add,
                        (ub, vb), (xcb,))
                for j in range(1, 4):
                    self.stt(xc[:, c, 0:N], uext[:, c, j:j + N], cw[j][:, c:c + 1], xc[:, c, 0:N], ALU.mult, ALU.add,
                             (ub, vb, xcb), (xcb,))
            self.cp("act", xcbf[:, :, 0:N], xc[:, :, 0:N], (xcb,), (xcbfb,))
            self.cp("act", hist, uext[:, :, N:N + 3], (ub,), (histb,))
            for c in range(8):
                k = sc[ci % 2]
                ci += 1
                n, hf = c // 2, c % 2
                b = self.next_bank("g")
                for kcl in range(2):
                    self.mm(self.bank[b][:, 0:N], Ga[:, 2 * n + kcl, hf * 128:(hf + 1) * 128], xcbf[:, 2 * n + kcl, 0:N],
                            kcl == 0, kcl == 1, (self.wb, xcbfb), (self.bankb[b],))
                self.act(k["r"][:, 0:N], self.bank[b][:, 0:N], AF.Sigmoid, (self.bankb[b], vb), (k["rb"],), bias=gab[:, c:c + 1])
                b = self.next_bank("g")
                for kcl in range(2):
                    self.mm(self.bank[b][:, 0:N], Gx[:, 2 * n + kcl, hf * 128:(hf + 1) * 128], xcbf[:, 2 * n + kcl, 0:N],
                            kcl == 0, kcl == 1, (self.wb, xcbfb), (self.bankb[b],))
                self.act(k["ii"][:, 0:N], self.bank[b][:, 0:N], AF.Sigmoid, (self.bankb[b], vb), (k["iib"],), bias=gxb[:, c:c + 1])
                self.act(k["a"][:, 0:N], k["r"][:, 0:N], AF.Exp, (k["rb"], vb), (k["ab"],), scale=m8[:, c:c + 1])
                self.act(k["a2"][:, 0:N], k["r"][:, 0:N], AF.Exp, (k["rb"], vb), (k["a2b"],), scale=m16[:, c:c + 1])
                self.ts("dve", k["a2"][:, 0:N], k["a2"][:, 0:N], -1.0, 1.0, ALU.mult, ALU.add, (k["a2b"],), (k["a2b"],))
                self.act(k["a2"][:, 0:N], k["a2"][:, 0:N], AF.Sqrt, (k["a2b"],), (k["a2b"],))
                self.tt("dve", k["ii"][:, 0:N], k["ii"][:, 0:N], xc[:, c, 0:N], ALU.mult, (k["iib"], xcb), (k["iib"],))
                self.tt("dve", k["ii"][:, 0:N], k["ii"][:, 0:N], k["a2"][:, 0:N], ALU.mult, (k["iib"], k["a2b"]), (k["iib"],))
                init = 0.0 if (s >= 0 and t == 0) else hprev[:, c:c + 1]
                aa, bbv, oo = k["a"][:, 0:N], k["ii"][:, 0:N], hs[:, c, 0:N]
                self.P.op("dve", (lambda e, aa=aa, bbv=bbv, oo=oo, init=init: e.tensor_tensor_scan(
                    out=oo, data0=aa, data1=bbv, initial=init, op0=ALU.mult, op1=ALU.add)),
                    (k["ab"], k["iib"], hpb), (hsb,))
                self.tt("dve", yin[:, c, 0:N], hs[:, c, 0:N], gate[:, c, 0:N], ALU.mult, (hsb, gateb), (yinb,))
            self.cp("act", hprev, hs[:, :, N - 1], (hsb,), (hpb,))
            for co in range(8):
                b = self.next_bank("g")
                for j in range(8):
                    self.mm(self.bank[b][:, 0:N], Wo[:, j, co * 128:(co + 1) * 128], yin[:, j, 0:N], j == 0, j == 7,
                            (self.wb, yinb), (self.bankb[b],))
                self.tt("dve", xT[:, co, 0:N], self.bank[b][:, 0:N], xT[:, co, 0:N], ALU.add, (self.bankb[b], xb), (xb,))
            self.store_x(tile, xT, xb)
            if s < 0 or t == 3:
                if s >= 0:
                    oc, oh = O["b_conv_p"][0, s], O["b_h_p"][0, s]
                else:
                    oc, oh = O["b_conv_s"][0, -1 - s], O["b_h_s"][0, -1 - s]
                self.cp("act", hstage.rearrange("p t c -> p c t"), hist, (histb,), (hstb,))
                for tt_ in range(3):
                    self.dma(oc[tt_].rearrange("(c p) -> p c", p=128), hstage[:, tt_, :], (hstb,), (), slow=True)
                self.dma(oh.rearrange("(c p) -> p c", p=128), hprev, (hpb,), (), slow=True)

    def pass_mlp(self, layer):
        I, O, S, ar = self.I, self.O, self.S, self.ar
        self.phase()
        self._pools = {"g": [0, 1, 2, 3], "y": [4, 5, 6, 7]}
        W1 = ar.alloc(BF16, 128, 8, 4 * D)
        W2 = ar.alloc(BF16, 128, 32, D)
        mark = ar.off
        stg = [ar.alloc(F32, 128, 2048) for _ in range(3)]
        stgb = [self.buf("stg") for _ in range(3)]
        self.load_weight(I["mlp_w1"][layer], W1, D, 4 * D, stg, stgb)
        self.load_weight(I["mlp_w2"][layer], W2, 4 * D, D, stg, stgb)
        self.P.barrier()
        ar.reset(mark)
        last = (layer == 3)
        nxb = 1 if last else 2
        xTl = [ar.alloc(F32, 128, 8, 512) for _ in range(nxb)]
        xbl = [self.buf("xT") for _ in range(nxb)]
        hT = ar.alloc(BF16, 128, 8, 512)
        hb = self.buf("hT")
        hid = ar.alloc(BF16, 128, 16, 512)
        hidb = [self.buf("hid") for _ in range(16)]
        sq = hid[:, 0:8, :]
        rs = ar.alloc(F32, 128, 512)
        rsb = self.buf("rs")
        rl = [ar.alloc(BF16, 128, 512) for _ in range(2)]
        rlb = [self.buf("rl") for _ in range(2)]
        if last:
            tm = ar.alloc(F32, 128, 4, D)
            tmb = self.buf("tm")
        gm = self.gmlp[:, layer, :]
        ri = 0
        hall = Buf("hid_all")
        tl_ = self.tiles()
        if nxb == 2:
            self.load_x(tl_[0], xTl[0], xbl[0], False)
        for ti_, tile in enumerate(tl_):
            col0, N, s, t = tile
            xT, xb = xTl[ti_ % nxb], xbl[ti_ % nxb]
            if nxb == 2:
                if ti_ + 1 < len(tl_):
                    self.load_x(tl_[ti_ + 1], xTl[(ti_ + 1) % 2], xbl[(ti_ + 1) % 2], False)
            else:
                self.load_x(tile, xT, xb, False)
            self.norm(xT, xb, N, gm, hT, hb, sq, hall, rs, rsb)
            for half in range(2):
                for fi in range(16):
                    f = half * 16 + fi
                    b = self.next_bank("g")
                    for kc in range(8):
                        self.mm(self.bank[b][:, 0:N], W1[:, kc, f * 128:(f + 1) * 128], hT[:, kc, 0:N], kc == 0, kc == 7,
                                (self.wb, hb), (self.bankb[b],))
                    r_, rb_ = rl[ri % 2], rlb[ri % 2]
                    ri += 1
                    self.act(r_[:, 0:N], self.bank[b][:, 0:N], AF.Relu, (self.bankb[b],), (rb_,))
                    self.tt("dve", hid[:, fi, 0:N], r_[:, 0:N], r_[:, 0:N], ALU.mult, (rb_,), (hall,))
                for co in range(8):
                    b = self.next_bank("y")
                    for fi in range(16):
                        f = half * 16 + fi
                        self.mm(self.bank[b][:, 0:N], W2[:, f, co * 128:(co + 1) * 128], hid[:, fi, 0:N], fi == 0, fi == 15,
                                (self.wb, hall), (self.bankb[b],))
                    self.tt("dve", xT[:, co, 0:N], self.bank[b][:, 0:N], xT[:, co, 0:N], ALU.add, (self.bankb[b], xb), (xb,))
            if not last:
                self.store_x(tile, xT, xb)
            else:
                self.norm(xT, xb, N, self.gfin, hT, hb, sq, hall, rs, rsb)
                for c in range(8):
                    self.stt(xT[:, c, 0:N], xT[:, c, 0:N], self.gfin[:, c:c + 1], rs[:, 0:N], ALU.mult, ALU.mult,
                             (xb, rsb, self.cb), (xb,))

                def rows(u, nt_):
                    if s >= 0:
                        r0 = t * 512 + u * 128
                        return [(O["y_prompt"][s, r0:r0 + 128, :], 0, 128)]
                    return [(O["y_sample"][0, :, :], 0, 16), (O["y_sample"][1, :, :], 16, 32)]
                self.out_tm(xT, xb, N, tm, tmb, rows)


_NC_CACHE = {}


def _consts():
    c = {}
    c["k_ident"] = np.eye(128, dtype=np.float32)
    c["k_jmat"] = np.ascontiguousarray(np.eye(128, dtype=np.float32)[::-1])
    c["k_ones"] = np.ones((128, 128), np.float32)
    bo = np.zeros((128, 128), np.float32)
    bo[0:64, 0:64] = 1.0
    bo[64:128, 64:128] = 1.0
    c["k_bones"] = bo
    rm = np.zeros((128, 128), np.float32)
    for p in range(128):
        d = p % 64
        if d < 32:
            rm[p + 32, p] = -1.0
        else:
            rm[p - 32, p] = 1.0
    c["k_rmat"] = rm
    op = np.zeros((128, 256), np.float32)
    op[:, 0:64] = 1.0
    op[:, 128 + 64:256] = 1.0
    c["k_onesp"] = op
    half = 32
    inv = (np.float32(10000.0) ** (-np.arange(half, dtype=np.float32) / np.float32(half))).astype(np.float32)
    pos = np.concatenate([np.arange(T), np.arange(T), PAST + np.arange(TS), PAST + np.arange(TS)]).astype(np.float32)
    ang = (pos[:, None] * inv[None, :]).astype(np.float32)
    cos = np.cos(ang).astype(np.float32)
    sin = np.sin(ang).astype(np.float32)
    fi = np.arange(128) % 32
    c["k_cos"] = np.ascontiguousarray(cos[:, fi].T)
    c["k_sin"] = np.ascontiguousarray(sin[:, fi].T)
    return c


def kernel(**inputs):
    if "nc" not in _NC_CACHE:
        _NC_CACHE["nc"] = K().build()
    nc = _NC_CACHE["nc"]
    consts = _consts()
    f = lambda a: np.ascontiguousarray(np.asarray(a, dtype=np.float32))
    in_maps = []
    shared = {}
    for nm in ["norm_mix_g", "norm_mlp_g", "norm_final_g", "a_w_in", "a_q_norm_g", "a_k_norm_g", "a_subln_g",
               "a_w_out", "b_w_in", "b_b_in", "b_conv_w", "b_conv_b", "b_gate_a_w", "b_gate_a_b", "b_gate_x_w",
               "b_gate_x_b", "b_lambda", "b_w_out", "c_w_in", "c_q_norm_g", "c_k_norm_g", "c_rel_bias", "c_w_out",
               "mlp_w1", "mlp_w2"]:
        shared[nm] = f(inputs[nm])
    shared["a_lambda"] = f(inputs["a_lambda"]).reshape(2, 256)
    shared.update(consts)
    for c in range(NCORES):
        m = dict(shared)
        sl = slice(2 * c, 2 * c + 2)
        m["x_prompt"] = f(inputs["x_prompt"][sl])
        m["x_sample"] = f(inputs["x_sample"][sl])
        m["cache_a_k"] = f(np.asarray(inputs["cache_a_k"])[:, sl].reshape(2, 2, PAST, D))
        m["cache_a_v"] = f(np.asarray(inputs["cache_a_v"])[:, sl].reshape(2, 2, PAST, D))
        m["state_b_conv"] = f(np.asarray(inputs["state_b_conv"])[:, sl])
        m["state_b_h"] = f(np.asarray(inputs["state_b_h"])[:, sl])
        m["cache_c_k"] = f(np.asarray(inputs["cache_c_k"])[:, sl].reshape(1, 2, 512, D))
        m["cache_c_v"] = f(np.asarray(inputs["cache_c_v"])[:, sl].reshape(1, 2, 512, D))
        in_maps.append(m)
    if KCORES < NCORES:
        res = run_bass_kernel_spmd(nc, in_maps[:KCORES], core_ids=list(range(KCORES)))
        R = list(res.results) + [res.results[0]] * (NCORES - KCORES)
    else:
        res = run_bass_kernel_spmd(nc, in_maps, core_ids=list(range(NCORES)))
        R = res.results
    if DEBUG:
        _NC_CACHE["res"] = R

    def cat(name, axis, shape):
        return np.concatenate([np.asarray(R[c][name]) for c in range(NCORES)], axis=axis).reshape(shape).astype(np.float32)
    B = 16
    return (
        cat("y_prompt", 0, (B, T, D)),
        cat("y_sample", 0, (B, TS, D)),
        cat("a_k_p", 1, (2, B, T, 8, 2, 64)),
        cat("a_v_p", 1, (2, B, T, 8, 128)),
        cat("a_k_s", 1, (2, B, TS, 8, 2, 64)),
        cat("a_v_s", 1, (2, B, TS, 8, 128)),
        cat("b_conv_p", 1, (1, B, 3, D)),
        cat("b_h_p", 1, (1, B, D)),
        cat("b_conv_s", 1, (1, B, 3, D)),
        cat("b_h_s", 1, (1, B, D)),
        cat("c_k_p", 1, (1, B, 512, 16, 64)),
        cat("c_v_p", 1, (1, B, 512, 16, 64)),
        cat("c_k_s", 1, (1, B, TS, 16, 64)),
        cat("c_v_s", 1, (1, B, TS, 16, 64)),
    )
```
